# Optimizing a Trainium2 kernel written in Bass

```python
import math
import jax, jax.numpy as jnp
from jax import lax
import numpy as np

D_MODEL = 1024
BATCH = 8
SEQ = 4096
DEPTH = 4

CHUNK = 64
Q_BLOCK = 128
N_A_LAYERS = DEPTH // 2
N_B_LAYERS = DEPTH - N_A_LAYERS
SSM_GROUP = 16
SSM_GROUPS = D_MODEL // SSM_GROUP
SSM_STATE = 64
DT_MIN = 0.001
DT_MAX = 0.1
N_HEADS = 16
QK_NOPE_DIM = 64
QK_ROPE_DIM = 32
V_HEAD_DIM = 64
Q_LORA_RANK = 256
KV_LORA_RANK = 256
ROPE_THETA = 10000.0
ATTN_SCALE = 1.0 / math.sqrt(QK_NOPE_DIM + QK_ROPE_DIM)
D_FF = ((8 * D_MODEL + 3 * 256 - 1) // (3 * 256)) * 256
EPS = 1e-6

kernel_name = "yoco_s5_mla_adaln_encoder"


def rms_norm(x, g):
    xf = x.astype(jnp.float32)
    y = xf * lax.rsqrt(jnp.mean(xf * xf, axis=-1, keepdims=True) + EPS)
    return (y * g.astype(jnp.float32)).astype(x.dtype)


def modulate(h, shift, scale):
    return h * (1.0 + scale[:, None, :]) + shift[:, None, :]


def rope_cos_sin(positions):
    inv = 1.0 / (ROPE_THETA ** (jnp.arange(0, QK_ROPE_DIM, 2, dtype=jnp.float32) / QK_ROPE_DIM))
    ang = positions.astype(jnp.float32)[..., None] * inv
    return jnp.cos(ang), jnp.sin(ang)


def apply_rope(x, cos, sin):
    shape = cos.shape[:2] + (1,) * (x.ndim - 3) + cos.shape[-1:]
    cos = cos.reshape(shape)
    sin = sin.reshape(shape)
    x1, x2 = jnp.split(x.astype(jnp.float32), 2, axis=-1)
    return jnp.concatenate([x1 * cos - x2 * sin, x1 * sin + x2 * cos], axis=-1).astype(x.dtype)


def _complex_linear_combine(e1, e2):
    a1r, a1i, b1r, b1i = e1
    a2r, a2i, b2r, b2i = e2
    ar = a1r * a2r - a1i * a2i
    ai = a1r * a2i + a1i * a2r
    br = a2r * b1r - a2i * b1i + b2r
    bi = a2r * b1i + a2i * b1r + b2i
    return (ar, ai, br, bi)


def s5_mixer(h, lam_re, lam_im, log_dt, b_re, b_im, c_re, c_im, d_skip, w_glu, b_glu):
    bsz, s_len, d = h.shape
    f32 = jnp.float32
    lr = lam_re.astype(f32)
    li = lam_im.astype(f32)
    dt = jnp.exp(log_dt.astype(f32))[:, None]
    mag = jnp.exp(lr * dt)
    ab_re = mag * jnp.cos(li * dt)
    ab_im = mag * jnp.sin(li * dt)
    den = lr * lr + li * li
    nr = ab_re - 1.0
    ni = ab_im
    f_re = (nr * lr + ni * li) / den
    f_im = (ni * lr - nr * li) / den
    br = b_re.astype(f32)
    bi = b_im.astype(f32)
    bb_re = f_re[..., None] * br - f_im[..., None] * bi
    bb_im = f_re[..., None] * bi + f_im[..., None] * br
    cr = c_re.astype(f32)
    ci = c_im.astype(f32)

    n_chunks = s_len // CHUNK
    u = h.astype(f32).reshape(bsz, n_chunks, CHUNK, SSM_GROUPS, SSM_GROUP).transpose(1, 0, 2, 3, 4)

    def chunk_step(carry, u_c):
        s_re, s_im = carry
        bu_re = jnp.einsum('blgp,gnp->blgn', u_c, bb_re)
        bu_im = jnp.einsum('blgp,gnp->blgn', u_c, bb_im)
        a_re = jnp.broadcast_to(ab_re, bu_re.shape)
        a_im = jnp.broadcast_to(ab_im, bu_im.shape)
        pa_re, pa_im, loc_re, loc_im = lax.associative_scan(
            _complex_linear_combine, (a_re, a_im, bu_re, bu_im), axis=1)
        st_re = loc_re + pa_re * s_re[:, None] - pa_im * s_im[:, None]
        st_im = loc_im + pa_re * s_im[:, None] + pa_im * s_re[:, None]
        y = jnp.einsum('blgn,gpn->blgp', st_re, cr) - jnp.einsum('blgn,gpn->blgp', st_im, ci)
        return (st_re[:, -1], st_im[:, -1]), y

    init = (jnp.zeros((bsz, SSM_GROUPS, SSM_STATE), f32), jnp.zeros((bsz, SSM_GROUPS, SSM_STATE), f32))
    _, y = lax.scan(chunk_step, init, u)
    y = y.transpose(1, 0, 2, 3, 4).reshape(bsz, s_len, d)
    y = (y + d_skip.astype(f32) * h.astype(f32)).astype(h.dtype)
    g = jax.nn.gelu(y)
    return g * jax.nn.sigmoid(g @ w_glu + b_glu)


def shared_mla_kv(hk, cos, sin, w_kv_a, kv_a_norm_g, w_kv_b, k_nope_norm_g, k_rope_norm_g):
    bsz, s_len, _ = hk.shape
    kv_a = hk @ w_kv_a
    c_kv, k_rope = jnp.split(kv_a, [KV_LORA_RANK], axis=-1)
    c_kv = rms_norm(c_kv, kv_a_norm_g)
    kv = (c_kv @ w_kv_b).reshape(bsz, s_len, N_HEADS, QK_NOPE_DIM + V_HEAD_DIM)
    k_nope, v = jnp.split(kv, [QK_NOPE_DIM], axis=-1)
    k_nope = rms_norm(k_nope, k_nope_norm_g)
    k_rope = apply_rope(rms_norm(k_rope, k_rope_norm_g), cos, sin)
    return k_nope, k_rope, v


def mla_attention(h, cos, sin, k_nope, k_rope, v, w_dq, q_norm_g, w_uq, q_nope_norm_g, q_rope_norm_g, w_o):
    bsz, s_len, _ = h.shape
    q = rms_norm(h @ w_dq, q_norm_g) @ w_uq
    q = q.reshape(bsz, s_len, N_HEADS, QK_NOPE_DIM + QK_ROPE_DIM)
    q_nope, q_rope = jnp.split(q, [QK_NOPE_DIM], axis=-1)
    q_nope = rms_norm(q_nope, q_nope_norm_g)
    q_rope = apply_rope(rms_norm(q_rope, q_rope_norm_g), cos, sin)
    n_blocks = s_len // Q_BLOCK
    qn_b = q_nope.reshape(bsz, n_blocks, Q_BLOCK, N_HEADS, QK_NOPE_DIM).transpose(1, 0, 2, 3, 4)
    qr_b = q_rope.reshape(bsz, n_blocks, Q_BLOCK, N_HEADS, QK_ROPE_DIM).transpose(1, 0, 2, 3, 4)
    key_chunk = jnp.arange(s_len) // CHUNK

    def block_attn(args):
        qn, qr, blk = args
        s = (jnp.einsum('bqhd,bkhd->bhqk', qn, k_nope, preferred_element_type=jnp.float32)
             + jnp.einsum('bqhr,bkr->bhqk', qr, k_rope, preferred_element_type=jnp.float32)) * ATTN_SCALE
        q_chunk = (blk * Q_BLOCK + jnp.arange(Q_BLOCK)) // CHUNK
        mask = q_chunk[:, None] >= key_chunk[None, :]
        s = jnp.where(mask[None, None], s, -1e30)
        p = jax.nn.softmax(s, axis=-1)
        return jnp.einsum('bhqk,bkhd->bqhd', p.astype(v.dtype), v)

    o = lax.map(block_attn, (qn_b, qr_b, jnp.arange(n_blocks)))
    o = o.transpose(1, 0, 2, 3, 4).reshape(bsz, s_len, N_HEADS * V_HEAD_DIM)
    return o @ w_o


def swiglu(h, w_gate, w_up, w_down):
    return (jax.nn.silu(h @ w_gate) * (h @ w_up)) @ w_down


def setup_inputs(seed: int = 0) -> dict:
    key = jax.random.key(seed)
    ks = iter(jax.random.split(key, 48))
    f32 = jnp.float32
    D, F, G, N, P = D_MODEL, D_FF, SSM_GROUPS, SSM_STATE, SSM_GROUP
    NA, NB = N_A_LAYERS, N_B_LAYERS

    def nrm(shape, scale):
        return jax.random.normal(next(ks), shape, f32) * scale

    def gain(shape):
        return 1.0 + 0.02 * jax.random.normal(next(ks), shape, f32)

    x = jax.random.normal(next(ks), (BATCH, SEQ, D), f32)
    c = jax.random.normal(next(ks), (BATCH, D), f32)
    offsets = jax.random.randint(next(ks), (BATCH, 1), 0, 4096, dtype=jnp.int32)
    positions = offsets + jnp.arange(SEQ, dtype=jnp.int32)[None, :]

    n_idx = jnp.arange(N, dtype=f32)
    inputs = {
        "x": x, "c": c, "positions": positions,
        "ada_w": nrm((DEPTH, D, 6 * D), 0.5 * D ** -0.5),
        "ada_b": nrm((DEPTH, 6 * D), 0.02),
        "norm1_g": gain((DEPTH, D)),
        "norm2_g": gain((DEPTH, D)),
        "ffn_w_gate": nrm((DEPTH, D, F), D ** -0.5),
        "ffn_w_up": nrm((DEPTH, D, F), D ** -0.5),
        "ffn_w_down": nrm((DEPTH, F, D), F ** -0.5),
        "s5_lam_re": -0.5 + 0.01 * jax.random.normal(next(ks), (NA, G, N), f32),
        "s5_lam_im": math.pi * n_idx[None, None, :] + 0.01 * jax.random.normal(next(ks), (NA, G, N), f32),
        "s5_log_dt": jax.random.uniform(next(ks), (NA, G), f32, math.log(DT_MIN), math.log(DT_MAX)),
        "s5_b_re": nrm((NA, G, N, P), P ** -0.5),
        "s5_b_im": nrm((NA, G, N, P), P ** -0.5),
        "s5_c_re": nrm((NA, G, P, N), N ** -0.5),
        "s5_c_im": nrm((NA, G, P, N), N ** -0.5),
        "s5_d": nrm((NA, D), 1.0),
        "s5_w_glu": nrm((NA, D, D), D ** -0.5),
        "s5_b_glu": nrm((NA, D), 0.02),
        "kv_ada_w": nrm((D, 2 * D), 0.5 * D ** -0.5),
        "kv_ada_b": nrm((2 * D,), 0.02),
        "kv_norm_g": gain((D,)),
        "w_kv_a": nrm((D, KV_LORA_RANK + QK_ROPE_DIM), D ** -0.5),
        "kv_a_norm_g": gain((KV_LORA_RANK,)),
        "w_kv_b": nrm((KV_LORA_RANK, N_HEADS * (QK_NOPE_DIM + V_HEAD_DIM)), KV_LORA_RANK ** -0.5),
        "k_nope_norm_g": gain((QK_NOPE_DIM,)),
        "k_rope_norm_g": gain((QK_ROPE_DIM,)),
        "mla_w_dq": nrm((NB, D, Q_LORA_RANK), D ** -0.5),
        "mla_q_norm_g": gain((NB, Q_LORA_RANK)),
        "mla_w_uq": nrm((NB, Q_LORA_RANK, N_HEADS * (QK_NOPE_DIM + QK_ROPE_DIM)), Q_LORA_RANK ** -0.5),
        "mla_q_nope_norm_g": gain((NB, QK_NOPE_DIM)),
        "mla_q_rope_norm_g": gain((NB, QK_ROPE_DIM)),
        "mla_w_o": nrm((NB, N_HEADS * V_HEAD_DIM, D), (N_HEADS * V_HEAD_DIM) ** -0.5),
    }
    return inputs


def reference(x, c, positions, ada_w, ada_b, norm1_g, norm2_g, ffn_w_gate, ffn_w_up, ffn_w_down,
              s5_lam_re, s5_lam_im, s5_log_dt, s5_b_re, s5_b_im, s5_c_re, s5_c_im, s5_d, s5_w_glu, s5_b_glu,
              kv_ada_w, kv_ada_b, kv_norm_g, w_kv_a, kv_a_norm_g, w_kv_b, k_nope_norm_g, k_rope_norm_g,
              mla_w_dq, mla_q_norm_g, mla_w_uq, mla_q_nope_norm_g, mla_q_rope_norm_g, mla_w_o):
    cos, sin = rope_cos_sin(positions)
    c_act = jax.nn.silu(c)
    k_nope = k_rope = v = None
    for l in range(DEPTH):
        shift1, scale1, gate1, shift2, scale2, gate2 = jnp.split(c_act @ ada_w[l] + ada_b[l], 6, axis=-1)
        if l == N_A_LAYERS:
            k_shift, k_scale = jnp.split(c_act @ kv_ada_w + kv_ada_b, 2, axis=-1)
            hk = modulate(rms_norm(x, kv_norm_g), k_shift, k_scale)
            k_nope, k_rope, v = shared_mla_kv(hk, cos, sin, w_kv_a, kv_a_norm_g, w_kv_b,
                                              k_nope_norm_g, k_rope_norm_g)
        h = modulate(rms_norm(x, norm1_g[l]), shift1, scale1)
        if l < N_A_LAYERS:
            mix = s5_mixer(h, s5_lam_re[l], s5_lam_im[l], s5_log_dt[l], s5_b_re[l], s5_b_im[l],
                           s5_c_re[l], s5_c_im[l], s5_d[l], s5_w_glu[l], s5_b_glu[l])
        else:
            j = l - N_A_LAYERS
            mix = mla_attention(h, cos, sin, k_nope, k_rope, v, mla_w_dq[j], mla_q_norm_g[j], mla_w_uq[j],
                                mla_q_nope_norm_g[j], mla_q_rope_norm_g[j], mla_w_o[j])
        x = x + gate1[:, None, :] * mix
        h = modulate(rms_norm(x, norm2_g[l]), shift2, scale2)
        x = x + gate2[:, None, :] * swiglu(h, ffn_w_gate[l], ffn_w_up[l], ffn_w_down[l])
    return x
```

```python
import numpy as np
import concourse.bass as bass
import concourse.mybir as mybir
from concourse.bass_utils import run_bass_kernel_spmd
from contextlib import ExitStack
import types


def _freeze(fn):
    if fn.__closure__ is None:
        return fn
    cells = []
    for c in fn.__closure__:
        try:
            cells.append(types.CellType(c.cell_contents))
        except ValueError:
            cells.append(c)
    return types.FunctionType(fn.__code__, fn.__globals__, fn.__name__, fn.__defaults__, tuple(cells))

F32 = mybir.dt.float32
BF16 = mybir.dt.bfloat16
I32 = mybir.dt.int32
AF = mybir.ActivationFunctionType
ALU = mybir.AluOpType
AX = mybir.AxisListType


class Prog:
    CE = ("pe", "act", "dve", "pool")
    NPOOL = 20

    def __init__(self, nc, es):
        self.nc = nc
        self.es = es
        self.q = {e: [] for e in self.CE + ("sp",)}
        self.sem = {e: es.enter_context(nc.semaphore("s_" + e)) for e in self.CE}
        self.cnt = {e: 0 for e in self.CE}
        self.seen = {e: {} for e in self.CE + ("sp",)}
        self.bufs = {}
        self.dpool = {}
        for qn in ("sp", "pool", "act"):
            self.dpool[qn] = [[es.enter_context(nc.semaphore("d_%s%d" % (qn, i))), 0]
                              for i in range(self.NPOOL)]
        self.dnext = {"sp": 0, "pool": 0, "act": 0}
        self.semobj = {}
        for e in self.CE:
            self.semobj[("e", e)] = self.sem[e]
        for qn in self.dpool:
            for i, (s, _) in enumerate(self.dpool[qn]):
                self.semobj[("d", qn, i)] = s
        self.out_waits = []
        self.n_inst = 0

    def _need(self, eng, reads, writes):
        need = {}

        def add(t):
            k, v = t
            if need.get(k, 0) < v:
                need[k] = v
        for k in reads:
            b = self.bufs.get(k)
            if b is not None and b["w"] is not None:
                add(b["w"])
        for k in writes:
            b = self.bufs.get(k)
            if b is not None:
                if b["w"] is not None:
                    add(b["w"])
                for t in b["r"].items():
                    add(t)
        return need

    def _emit_waits(self, eng, need):
        for k, v in need.items():
            if eng == "pe" and k == ("e", "pe"):
                continue
            if self.seen[eng].get(k, 0) >= v:
                continue
            self.seen[eng][k] = v
            s = self.semobj[k]
            self.q[eng].append(lambda e, s=s, v=v: e.wait_ge(s, v))
            self.n_inst += 1

    def _record(self, tag, reads, writes):
        for k in reads:
            b = self.bufs.setdefault(k, {"w": None, "r": {}})
            if b["r"].get(tag[0], 0) < tag[1]:
                b["r"][tag[0]] = tag[1]
        for k in writes:
            self.bufs[k] = {"w": tag, "r": {}}

    def op(self, eng, fn, reads=(), writes=()):
        fn = _freeze(fn)
        need = self._need(eng, reads, writes)
        self._emit_waits(eng, need)
        self.cnt[eng] += 1
        c = self.cnt[eng]
        s = self.sem[eng]
        self.q[eng].append(lambda e, fn=fn, s=s: fn(e).then_inc(s, 1))
        self.n_inst += 1
        self._record((("e", eng), c), reads, writes)
        return c

    def dma(self, qn, out, in_, reads=(), writes=(), is_output=False, **kw):
        need = self._need(qn, reads, writes)
        i = self.dnext[qn]
        self.dnext[qn] = (i + 1) % self.NPOOL
        ent = self.dpool[qn][i]
        key = ("d", qn, i)
        if ent[1] > 0:
            if need.get(key, 0) < ent[1]:
                need[key] = ent[1]
        self._emit_waits(qn, need)
        ent[1] += 16
        v = ent[1]
        s = ent[0]
        self.q[qn].append(lambda e, s=s, out=out, in_=in_, kw=kw: e.dma_start(out=out, in_=in_, **kw).then_inc(s, 16))
        self.n_inst += 1
        self._record((key, v), reads, writes)
        if is_output:
            self.out_waits.append((key, v))

    def barrier(self):
        for e in self.CE + ("sp",):
            need = {("e", f): self.cnt[f] for f in self.CE if self.cnt[f] > 0}
            for qn in self.dpool:
                for i, ent in enumerate(self.dpool[qn]):
                    if ent[1] > 0:
                        need[("d", qn, i)] = ent[1]
            if e == "pe":
                need.pop(("e", "pe"), None)
            self._emit_waits(e, need)

    def finish(self):
        need = {}
        for k, v in self.out_waits:
            if need.get(k, 0) < v:
                need[k] = v
        self._emit_waits("sp", need)
        self.barrier()
        nc = self.nc
        with nc.Block() as block:
            @block.tensor
            def _(e):
                for f in self.q["pe"]:
                    f(e)

            @block.scalar
            def _(e):
                for f in self.q["act"]:
                    f(e)

            @block.vector
            def _(e):
                for f in self.q["dve"]:
                    f(e)

            @block.gpsimd
            def _(e):
                for f in self.q["pool"]:
                    f(e)

            @block.sync
            def _(e):
                for f in self.q["sp"]:
                    f(e)

import math

S = 4096
NT = 8
TT = 512
EPS = 1e-6
TWO_PI = 2.0 * math.pi
ATTN_SCALE = 1.0 / math.sqrt(96.0)
ARENA_F = 19480


class Arena:
    def __init__(self, ap, off=0):
        self.ap = ap
        self.off = off

    def f32(self, n):
        v = self.ap[:, self.off:self.off + n]
        self.off += n
        assert self.off <= ARENA_F, self.off
        return v

    def bf16(self, n):
        m = (n + 1) // 2
        v = self.ap[:, self.off:self.off + m].bitcast(BF16)
        self.off += m
        assert self.off <= ARENA_F, self.off
        return v[:, 0:n]

    def i32(self, n):
        v = self.ap[:, self.off:self.off + n].bitcast(I32)
        self.off += n
        assert self.off <= ARENA_F, self.off
        return v


def build(n_layers=4, ffn_layers=(0, 1, 2, 3), mix_layers=(0, 1, 2, 3), prologue=True, s5_on=True, mla_stage=3, n_hp=8, n_qt=NT):
    nc = bass.Bass("TRN2", target_bir_lowering=False)

    def D(name, shape, dt=F32, kind="ExternalInput"):
        return nc.dram_tensor(name, list(shape), dt, kind=kind).ap()

    x_d = D("x", [S, 1024])
    out_d = D("out", [S, 1024], kind="ExternalOutput")
    cT_d = D("cT", [128, 8])
    ada_w = D("ada_w", [4, 1024, 6144])
    ada_b = D("ada_b", [128, 4 * 48])
    g12_d = D("g12", [128, 4 * 2 * 8])
    wg_d = D("wg", [4, 1024, 2816])
    wu_d = D("wu", [4, 1024, 2816])
    wd_d = D("wd", [4, 2816, 1024])
    s5p_d = D("s5p", [2, 128, 96])
    s5b_d = D("s5b", [2, 128, 2048])
    s5c_d = D("s5c", [2, 128, 2048])
    s5d_d = D("s5d", [128, 16])
    wglu_d = D("wglu", [2, 1024, 1024])
    bglu_d = D("bglu", [128, 16])
    kvaw_d = D("kvaw", [1024, 2048])
    kvab_d = D("kvab", [128, 16])
    kvg_d = D("kvg", [128, 8])
    wkva_d = D("wkva", [1024, 288])
    wkvar_d = D("wkvar", [1024, 32])
    kvag_d = D("kvag", [128, 2])
    wkvb_d = D("wkvb", [256, 2048])
    kg_d = D("kg", [96, 2])
    wdq_d = D("wdq", [2, 1024, 256])
    qng_d = D("qng", [128, 4])
    wuq_d = D("wuq", [2, 256, 1536])
    wuqr_d = D("wuqr", [2, 256, 512])
    qg_d = D("qg", [96, 4])
    wo_d = D("wo", [2, 1024, 1024])
    ident_d = D("ident", [128, 128])
    bq_d = D("bq", [96, 96])
    cst_d = D("cst", [96, 2])
    pos_d = D("pos", [96, S], I32)
    tab_d = D("ropetab", [96, 2, S], kind="Internal")
    wgs_d = D("wgs", [4, 22, 128, 1024], BF16, kind="Internal")
    wus_d = D("wus", [4, 22, 128, 1024], BF16, kind="Internal")
    wds_d = D("wds", [4, 16, 128, 1408], BF16, kind="Internal")
    gtd_d = D("s5gt", [2, 8, 128, 2048], BF16, kind="Internal")
    wtd_d = D("s5wt", [2, 128, 16384], BF16, kind="Internal")
    ktd_d = D("s5kt", [2, 8, 128, 1024], BF16, kind="Internal")

    es = ExitStack()
    with es:
        es.enter_context(nc.allow_low_precision("bf16 matmul operands with fp32 PSUM accumulation"))
        P = Prog(nc, es)

        def SB(name, shape, dt=F32):
            return es.enter_context(nc.sbuf_tensor(name, list(shape), dt))

        xT = SB("xT", [128, 8, S])
        AR = SB("arena", [128, ARENA_F])
        pb = [es.enter_context(nc.psum_tensor("pb%d" % i, [128, 512], F32)) for i in range(8)]
        ident = SB("identsb", [128, 128])
        onesD = SB("onesD", [128, 128], BF16)
        ones256 = SB("ones256", [128, 128], BF16)
        bq = SB("bqsb", [96, 96], BF16)
        MOD = SB("MOD", [128, 4 * 48])
        KVM = SB("KVM", [128, 16])
        ADB = SB("ADB", [128, 4 * 48])
        G12 = SB("G12", [128, 64])
        AA = SB("AA", [128, 64])
        KVB = SB("KVB", [128, 16])
        KVG = SB("KVG", [128, 8])
        AK = SB("AK", [128, 8])
        S5D = SB("S5D", [128, 16])
        BGL = SB("BGL", [128, 16])
        KVAG = SB("KVAG", [128, 2])
        KG = SB("KG", [96, 2])
        QNG = SB("QNG", [128, 4])
        QG = SB("QG", [96, 4])
        CST = SB("CST", [96, 2])
        cact = SB("cact", [128, 8])
        cab = SB("cab", [128, 8], BF16)
        EPSC = SB("EPSC", [128, 1])

        PB = lambda i: ("pb", i)

        def mod_ap(l, kind, c):
            j = l * 48 + kind * 8 + c
            return MOD[:, j:j + 1]

        P.op("dve", lambda e: e.memset(onesD[:], 1.0 / 1024.0), writes=["onesD"])
        P.op("dve", lambda e: e.memset(ones256[:], 1.0 / 256.0), writes=["ones256"])
        P.op("dve", lambda e: e.memset(EPSC[:], EPS), writes=["EPSC"])
        for (dst, src, key) in [(ident, ident_d, "ident"), (ADB, ada_b, "ADB"), (G12, g12_d, "G12"), (KVB, kvab_d, "KVB"),
                                (KVG, kvg_d, "KVG"), (S5D, s5d_d, "S5D"), (BGL, bglu_d, "BGL"), (KVAG, kvag_d, "KVAG"),
                                (KG, kg_d, "KG"), (QNG, qng_d, "QNG"), (QG, qg_d, "QG"), (CST, cst_d, "CST"), (cact, cT_d, "cact")]:
            P.dma("sp", dst[:], src, writes=[key])
        P.dma("pool", bq[:], bq_d, writes=["bq"])
        P.op("act", lambda e: e.activation(cab[:], cact[:], AF.Silu), reads=["cact"], writes=["cab"])

        ar = Arena(AR)
        wbuf = [ar.bf16(4096).rearrange("p (k f) -> p k f", f=512) for _ in range(2)]
        nd = 0
        for l in range(4 if prologue else 0):
            for pc in range(12):
                b = nd % 2
                nd += 1
                P.dma("pool", wbuf[b], ada_w[l].rearrange("(k p) f -> p k f", p=128)[:, :, pc * 512:(pc + 1) * 512], writes=[("wb", b)])
                for j in range(4):
                    col = pc * 4 + j
                    for k in range(8):
                        P.op("pe", lambda e, b=b, j=j, k=k, col=col: e.matmul(pb[1][:, col:col + 1], wbuf[b][:, k, j * 128:(j + 1) * 128], cab[:, k:k + 1], start=(k == 0), stop=(k == 7)),
                             reads=[("wb", b), "cab"], writes=[PB(1)])
            P.op("dve", lambda e, l=l: e.tensor_tensor(MOD[:, l * 48:(l + 1) * 48], pb[1][:, 0:48], ADB[:, l * 48:(l + 1) * 48], ALU.add), reads=[PB(1), "ADB"], writes=["MOD"])
        for pc in range(4 if prologue else 0):
            b = nd % 2
            nd += 1
            P.dma("pool", wbuf[b], kvaw_d.rearrange("(k p) f -> p k f", p=128)[:, :, pc * 512:(pc + 1) * 512], writes=[("wb", b)])
            for j in range(4):
                col = pc * 4 + j
                for k in range(8):
                    P.op("pe", lambda e, b=b, j=j, k=k, col=col: e.matmul(pb[1][:, col:col + 1], wbuf[b][:, k, j * 128:(j + 1) * 128], cab[:, k:k + 1], start=(k == 0), stop=(k == 7)),
                         reads=[("wb", b), "cab"], writes=[PB(1)])
        P.op("dve", lambda e: e.tensor_tensor(KVM[:], pb[1][:, 0:16], KVB[:], ALU.add), reads=[PB(1), "KVB"], writes=["KVM"])
        for l in range(4):
            for n in range(2):
                o = l * 16 + n * 8
                sc = l * 48 + (1 + 3 * n) * 8
                P.op("dve", lambda e, o=o, sc=sc: e.scalar_tensor_tensor(AA[:, o:o + 8], MOD[:, sc:sc + 8], 1.0, G12[:, o:o + 8], ALU.add, ALU.mult),
                     reads=["MOD", "G12"], writes=["AA"])
        P.op("dve", lambda e: e.scalar_tensor_tensor(AK[:], KVM[:, 8:16], 1.0, KVG[:], ALU.add, ALU.mult), reads=["KVM", "KVG"], writes=["AK"])
        P.barrier()

        def cast_ffn(l):
            wgv = wg_d[l].rearrange("(k p) f -> p k f", p=128)
            wuv = wu_d[l].rearrange("(k p) f -> p k f", p=128)
            wdv = wd_d[l].rearrange("(j p) d -> p j d", p=128)
            for hf in range(2):
                for jj in range(11):
                    j = hf * 11 + jj
                    P.dma("pool", wgs_d[l][j].rearrange("p (k f) -> p k f", f=128), wgv[:, :, j * 128:(j + 1) * 128], writes=[("wc", l, "g", j)])
                    P.dma("pool", wus_d[l][j].rearrange("p (k f) -> p k f", f=128), wuv[:, :, j * 128:(j + 1) * 128], writes=[("wc", l, "u", j)])
                for dc in range(8):
                    P.dma("pool", wds_d[l][hf * 8 + dc].rearrange("p (j d) -> p j d", d=128), wdv[:, hf * 11:(hf + 1) * 11, dc * 128:(dc + 1) * 128], writes=[("wc", l, "d", hf * 8 + dc)])

        cast_ffn(0)
        ar = Arena(AR)
        stg = [ar.f32(1024) for _ in range(2)]
        for t in range(32):
            b = t % 2
            P.dma("sp", stg[b], x_d[t * 128:(t + 1) * 128, :], writes=[("stg", b)])
            for hf in range(2):
                bk = 2 + b * 2 + hf
                for j in range(4):
                    c = hf * 4 + j
                    P.op("pe", lambda e, bk=bk, j=j, c=c, b=b: e.matmul(pb[bk][:, j * 128:(j + 1) * 128], stg[b][:, c * 128:(c + 1) * 128], ident[:], start=True, stop=True, is_transpose=True),
                         reads=[("stg", b), "ident"], writes=[PB(bk)])
                eng = "dve" if hf == 0 else "act"
                dst = xT[:, hf * 4:hf * 4 + 4, t * 128:(t + 1) * 128]
                src = pb[bk][:, :].rearrange("p (a b) -> p a b", b=128)
                if eng == "dve":
                    P.op("dve", lambda e, dst=dst, src=src: e.tensor_copy(dst, src), reads=[PB(bk)], writes=[("x", t // 4, hf * 4 + j_) for j_ in range(4)])
                else:
                    P.op("act", lambda e, dst=dst, src=src: e.copy(dst, src), reads=[PB(bk)], writes=[("x", t // 4, hf * 4 + j_) for j_ in range(4)])
        P.barrier()

        def rstd_from_psum(bank, rs, prange=slice(0, 128), key="rs", lntmp=None, lnkey=None):
            if lntmp is not None:
                P.op("act", lambda e: e.activation(lntmp[prange, :], pb[bank][prange, :], AF.Ln, bias=EPSC[prange, 0:1], scale=1.0), reads=[PB(bank), "EPSC"], writes=[lnkey])
                P.op("act", lambda e: e.activation(rs[prange, :], lntmp[prange, :], AF.Exp, scale=-0.5), reads=[lnkey], writes=[key])
                return
            P.op("dve", lambda e: e.tensor_scalar_add(rs[prange, :], pb[bank][prange, :], EPS), reads=[PB(bank)], writes=[key])
            P.op("act", lambda e: e.sqrt(rs[prange, :], rs[prange, :]), reads=[key], writes=[key])
            P.op("dve", lambda e: e.reciprocal(rs[prange, :], rs[prange, :]), reads=[key], writes=[key])

        def norm_mod(t, a_of_c, b_of_c, h, sq, rs, tmp, bank=7, perm=False, hkey="h"):
            ts = slice(t * TT, (t + 1) * TT)
            for c in range(8):
                P.op("act", lambda e, c=c: e.activation(sq[c % 2], xT[:, c, ts], AF.Square), reads=[("x", t, c)], writes=[("sq", c % 2)])
                P.op("pe", lambda e, c=c: e.matmul(pb[bank][:, :], onesD[:], sq[c % 2], start=(c == 0), stop=(c == 7)), reads=[("sq", c % 2), "onesD"], writes=[PB(bank)])
            rstd_from_psum(bank, rs)
            for c in range(8):
                P.op("dve", lambda e, c=c: e.scalar_tensor_tensor(tmp[c % 2], xT[:, c, ts], a_of_c(c), rs, ALU.mult, ALU.mult), reads=[("x", t, c), "rs"], writes=[("tmp", c % 2)])
                if perm:
                    P.op("act", lambda e, c=c: e.activation(h[:, c].rearrange("p t m -> p m t"), tmp[c % 2].rearrange("p (m t) -> p m t", t=8), AF.Identity, bias=b_of_c(c), scale=1.0), reads=[("tmp", c % 2)], writes=[hkey])
                else:
                    P.op("act", lambda e, c=c: e.activation(h[:, c, :], tmp[c % 2], AF.Identity, bias=b_of_c(c), scale=1.0), reads=[("tmp", c % 2)], writes=["h"])

        def ffn(l, t, ar0):
            ar = Arena(AR, ar0)
            h = ar.bf16(4096).rearrange("p (k n) -> p k n", n=512)
            act = ar.bf16(11 * 512).rearrange("p (k n) -> p k n", n=512)
            sq = [ar.bf16(512) for _ in range(2)]
            rs = ar.f32(512)
            tmp = [ar.f32(512) for _ in range(2)]
            sg = [ar.f32(512) for _ in range(2)]
            wgb = [ar.bf16(1024).rearrange("p (k f) -> p k f", f=128) for _ in range(2)]
            wub = [ar.bf16(1024).rearrange("p (k f) -> p k f", f=128) for _ in range(2)]
            wdb = [ar.bf16(11 * 128).rearrange("p (j d) -> p j d", d=128) for _ in range(2)]
            ts = slice(t * TT, (t + 1) * TT)
            norm_mod(t, lambda c: AA[:, l * 16 + 8 + c:l * 16 + 9 + c], lambda c: mod_ap(l, 3, c), h, sq, rs, tmp)
            for hf in range(2):
                for jj in range(11):
                    j = hf * 11 + jj
                    b = jj % 2
                    P.dma("sp", wgb[b].rearrange("p k f -> p (k f)"), wgs_d[l][j], reads=[("wc", l, "g", j)], writes=[("wg", b)])
                    P.dma("sp", wub[b].rearrange("p k f -> p (k f)"), wus_d[l][j], reads=[("wc", l, "u", j)], writes=[("wu", b)])
                    for k in range(8):
                        P.op("pe", lambda e, b=b, k=k: e.matmul(pb[b][:, :], wgb[b][:, k, :], h[:, k, :], start=(k == 0), stop=(k == 7)), reads=[("wg", b), "h"], writes=[PB(b)])
                    for k in range(8):
                        P.op("pe", lambda e, b=b, k=k: e.matmul(pb[2 + b][:, :], wub[b][:, k, :], h[:, k, :], start=(k == 0), stop=(k == 7)), reads=[("wu", b), "h"], writes=[PB(2 + b)])
                    P.op("act", lambda e, b=b: e.activation(sg[b], pb[b][:, :], AF.Silu), reads=[PB(b)], writes=[("sg", b)])
                    P.op("dve", lambda e, b=b, jj=jj: e.tensor_tensor(act[:, jj, :], sg[b], pb[2 + b][:, :], ALU.mult), reads=[("sg", b), PB(2 + b)], writes=[("act", jj)])
                for dc in range(8):
                    b = dc % 2
                    P.dma("sp", wdb[b].rearrange("p j d -> p (j d)"), wds_d[l][hf * 8 + dc], reads=[("wc", l, "d", hf * 8 + dc)], writes=[("wd", b)])
                    for jj in range(11):
                        P.op("pe", lambda e, b=b, jj=jj: e.matmul(pb[4 + b][:, :], wdb[b][:, jj, :], act[:, jj, :], start=(jj == 0), stop=(jj == 10)), reads=[("wd", b), ("act", jj)], writes=[PB(4 + b)])
                    P.op("dve", lambda e, b=b, dc=dc: e.scalar_tensor_tensor(xT[:, dc, ts], pb[4 + b][:, :], mod_ap(l, 5, dc), xT[:, dc, ts], ALU.mult, ALU.add),
                         reads=[PB(4 + b), ("x", t, dc)], writes=[("x", t, dc)])

        def rsin(dst, src, n, t1, ti, prange=slice(0, 128), key="rsin"):
            P.op("dve", lambda e: e.tensor_scalar_mul(t1[prange, 0:n], src, 1.0 / TWO_PI), reads=[key + "s"], writes=[key + "t"])
            P.op("dve", lambda e: e.tensor_copy(ti[prange, 0:n], t1[prange, 0:n]), reads=[key + "t"], writes=[key + "i"])
            P.op("dve", lambda e: e.tensor_copy(t1[prange, 0:n], ti[prange, 0:n]), reads=[key + "i"], writes=[key + "t"])
            P.op("dve", lambda e: e.scalar_tensor_tensor(t1[prange, 0:n], t1[prange, 0:n], -TWO_PI, src, ALU.mult, ALU.add), reads=[key + "t", key + "s"], writes=[key + "t"])
            P.op("act", lambda e: e.activation(dst, t1[prange, 0:n], AF.Sin), reads=[key + "t"], writes=[key + "d"])

        def s5_layer(l):
            ar = Arena(AR)
            LK = ar.f32(32 * 6 * 2).rearrange("p (a j r) -> p a j r", j=6, r=2)
            CAR = ar.f32(64).rearrange("p (a r) -> p a r", r=2)
            base = ar.off
            lam = ar.f32(96)
            P.dma("sp", lam, s5p_d[l], writes=["lam"])
            BB = ar.f32(2048).rearrange("p (r a b) -> p r a b", r=2, b=32)
            CC = ar.f32(2048).rearrange("p (r a b) -> p r a b", r=2, b=32)
            P.dma("sp", BB, s5b_d[l].rearrange("p (r a b) -> p r a b", r=2, b=32), writes=["BB"])
            P.dma("sp", CC, s5c_d[l].rearrange("p (r a b) -> p r a b", r=2, b=32), writes=["CC"])
            FB = ar.f32(2048).rearrange("p (r a b) -> p r a b", r=2, b=32)
            XBf = ar.f32(2048)
            XB = XBf.rearrange("p (r a b) -> p r a b", r=2, b=32)
            TB = ar.f32(1024).rearrange("p (a b) -> p a b", b=32)
            CCn = ar.f32(1024).rearrange("p (a b) -> p a b", b=32)
            PW = ar.f32(32 * 9 * 2).rearrange("p (a k r) -> p a k r", k=9, r=2)
            GTs = ar.bf16(2048).rearrange("p (c r m) -> p c r m", r=2, m=128)
            WTs = ar.bf16(2048).rearrange("p (r a b) -> p r a b", r=2, b=32)
            KTs = ar.bf16(1024).rearrange("p (c b) -> p c b", b=128)
            v = [ar.f32(32) for _ in range(16)]
            ti = ar.i32(32)
            dt, aa, mag, ang, sn, cs, are, aim, nr, den, fre, fim, t0, t1, ang2, t2 = v
            lr, li, ld = lam[:, 0:32], lam[:, 32:64], lam[:, 64:96]
            K = "s5prep"

            def tt(out, a, b, op):
                P.op("dve", lambda e: e.tensor_tensor(out, a, b, op), reads=[K, "lam", "BB", "CC"], writes=[K])
            P.op("act", lambda e: e.activation(dt, ld, AF.Exp), reads=["lam"], writes=[K])
            tt(aa, lr, dt, ALU.mult)
            P.op("act", lambda e: e.activation(mag, aa, AF.Exp), reads=[K], writes=[K])
            tt(ang, li, dt, ALU.mult)
            P.op("dve", lambda e: e.tensor_scalar_add(ang2, ang, 0.5 * math.pi), reads=[K], writes=[K])

            def rs_(dst, src):
                P.op("dve", lambda e: e.tensor_scalar_mul(t1, src, 1.0 / TWO_PI), reads=[K], writes=[K])
                P.op("dve", lambda e: e.tensor_copy(ti, t1), reads=[K], writes=[K])
                P.op("dve", lambda e: e.tensor_copy(t1, ti), reads=[K], writes=[K])
                P.op("dve", lambda e: e.scalar_tensor_tensor(t1, t1, -TWO_PI, src, ALU.mult, ALU.add), reads=[K], writes=[K])
                P.op("act", lambda e: e.activation(dst, t1, AF.Sin), reads=[K], writes=[K])
            rs_(sn, ang)
            rs_(cs, ang2)
            tt(are, mag, cs, ALU.mult)
            tt(aim, mag, sn, ALU.mult)
            P.op("dve", lambda e: e.tensor_scalar_add(nr, are, -1.0), reads=[K], writes=[K])
            tt(den, lr, lr, ALU.mult)
            tt(t0, li, li, ALU.mult)
            tt(den, den, t0, ALU.add)
            P.op("dve", lambda e: e.reciprocal(den, den), reads=[K], writes=[K])
            tt(fre, nr, lr, ALU.mult)
            tt(t0, aim, li, ALU.mult)
            tt(fre, fre, t0, ALU.add)
            tt(fre, fre, den, ALU.mult)
            tt(fim, aim, lr, ALU.mult)
            tt(t0, nr, li, ALU.mult)
            tt(fim, fim, t0, ALU.subtract)
            tt(fim, fim, den, ALU.mult)
            P.op("dve", lambda e: e.memset(PW[:, :, 0, 0], 1.0), reads=[K], writes=[K])
            P.op("dve", lambda e: e.memset(PW[:, :, 0, 1], 0.0), reads=[K], writes=[K])
            for k in range(1, 9):
                pr_, pi_ = PW[:, :, k - 1, 0], PW[:, :, k - 1, 1]
                tt(t0, pr_, are, ALU.mult)
                tt(t2, pi_, aim, ALU.mult)
                tt(PW[:, :, k, 0], t0, t2, ALU.subtract)
                tt(t0, pr_, aim, ALU.mult)
                tt(t2, pi_, are, ALU.mult)
                tt(PW[:, :, k, 1], t0, t2, ALU.add)
            P.op("dve", lambda e: e.tensor_copy(LK[:, :, 0, :], PW[:, :, 8, :]), reads=[K], writes=[K])
            for j in range(1, 6):
                pr_, pi_ = LK[:, :, j - 1, 0], LK[:, :, j - 1, 1]
                tt(t0, pr_, pr_, ALU.mult)
                tt(t2, pi_, pi_, ALU.mult)
                tt(LK[:, :, j, 0], t0, t2, ALU.subtract)
                tt(t0, pr_, pi_, ALU.mult)
                P.op("dve", lambda e, j=j: e.tensor_scalar_mul(LK[:, :, j, 1], t0, 2.0), reads=[K], writes=[K])
            frb = fre.unsqueeze(2).to_broadcast([128, 32, 32])
            fib = fim.unsqueeze(2).to_broadcast([128, 32, 32])
            tt(FB[:, 0], BB[:, 0], frb, ALU.mult)
            tt(TB, BB[:, 1], fib, ALU.mult)
            tt(FB[:, 0], FB[:, 0], TB, ALU.subtract)
            tt(FB[:, 1], BB[:, 1], frb, ALU.mult)
            tt(TB, BB[:, 0], fib, ALU.mult)
            tt(FB[:, 1], FB[:, 1], TB, ALU.add)
            P.op("dve", lambda e: e.tensor_scalar_mul(CCn, CC[:, 1], -1.0), reads=["CC"], writes=[K])
            gtv = gtd_d[l].rearrange("c p (t r m) -> p c t r m", t=8, r=2)
            ktv = ktd_d[l].rearrange("c p (k b) -> p c k b", b=128)
            P.op("dve", lambda e: e.memset(KTs, 0.0), writes=["KTs"])
            for k in range(8):
                prb = PW[:, :, k, 0:1].to_broadcast([128, 32, 32])
                pib = PW[:, :, k, 1:2].to_broadcast([128, 32, 32])
                tt(XB[:, 0], FB[:, 0], prb, ALU.mult)
                tt(TB, FB[:, 1], pib, ALU.mult)
                tt(XB[:, 0], XB[:, 0], TB, ALU.subtract)
                tt(XB[:, 1], FB[:, 1], prb, ALU.mult)
                tt(TB, FB[:, 0], pib, ALU.mult)
                tt(XB[:, 1], XB[:, 1], TB, ALU.add)
                for c in range(8):
                    for r in range(2):
                        bk = (c * 2 + r) % 2
                        P.op("pe", lambda e, c=c, r=r, bk=bk: e.matmul(pb[bk][:, 0:128], XBf[:, r * 1024 + c * 128:r * 1024 + (c + 1) * 128], ident[:], start=True, stop=True, is_transpose=True),
                             reads=[K, "ident"], writes=[PB(bk)])
                        P.op("act", lambda e, c=c, r=r, bk=bk: e.copy(GTs[:, c, r, :], pb[bk][:, 0:128]), reads=[PB(bk)], writes=["GTs"])
                P.dma("sp", gtv[:, :, 7 - k, :, :], GTs, reads=["GTs"], writes=[("GTd", l)])
                for pr in range(32):
                    pl = pr % 4
                    c = pr // 4
                    P.op("pe", lambda e, pr=pr, pl=pl, c=c: e.matmul(pb[2][32 * pl:32 * pl + 32, c * 32:(c + 1) * 32], XB[:, 0, pr, :], CC[:, 0, pr, :], start=True, stop=False, tile_position=(0, 32 * pl)),
                         reads=[K, "CC"], writes=[PB(2)])
                    P.op("pe", lambda e, pr=pr, pl=pl, c=c: e.matmul(pb[2][32 * pl:32 * pl + 32, c * 32:(c + 1) * 32], XB[:, 1, pr, :], CCn[:, pr, :], start=False, stop=True, tile_position=(0, 32 * pl)),
                         reads=[K, "CC"], writes=[PB(2)])
                for pl in range(4):
                    P.op("act", lambda e, pl=pl: e.copy(KTs[32 * pl:32 * pl + 32, :, 32 * pl:32 * pl + 32], pb[2][32 * pl:32 * pl + 32, 0:256].rearrange("p (c b) -> p c b", b=32)), reads=[PB(2)], writes=["KTs"])
                P.dma("sp", ktv[:, :, k, :], KTs, reads=["KTs"], writes=[("KTd", l)])
            for t in range(8):
                prb = PW[:, :, t + 1, 0:1].to_broadcast([128, 32, 32])
                pib = PW[:, :, t + 1, 1:2].to_broadcast([128, 32, 32])
                tt(XB[:, 0], CC[:, 0], prb, ALU.mult)
                tt(TB, CC[:, 1], pib, ALU.mult)
                tt(XB[:, 0], XB[:, 0], TB, ALU.subtract)
                tt(XB[:, 1], CCn, prb, ALU.mult)
                tt(TB, CC[:, 0], pib, ALU.mult)
                tt(XB[:, 1], XB[:, 1], TB, ALU.subtract)
                P.op("act", lambda e: e.copy(WTs, XB), reads=[K], writes=["WTs"])
                P.dma("sp", wtd_d[l][:, t * 2048:(t + 1) * 2048], WTs.rearrange("p r a b -> p (r a b)"), reads=["WTs"], writes=[("WTd", l)])
            P.op("dve", lambda e: e.memset(CAR, 0.0), writes=[("CAR", c) for c in range(8)])
            P.barrier()
            import os as _os
            if _os.environ.get("S5DBG") == "prep":
                return
            for t in range(NT):
                ar = Arena(AR, base)
                hbuf = [ar.bf16(4096).rearrange("p (k t m) -> p k t m", t=8, m=64) for _ in range(2)]
                h = hbuf[t % 2]
                HK = ("h", t % 2)
                gT = ar.bf16(4096).rearrange("p (k n) -> p k n", n=512)
                sq = [ar.bf16(512) for _ in range(2)]
                rs = ar.f32(512)
                tmp = [ar.f32(512) for _ in range(2)]
                GTc = [ar.bf16(2048).rearrange("p (t r m) -> p t r m", t=8, r=2) for _ in range(2)]
                WTc = [ar.bf16(2048).rearrange("p (t r a b) -> p t r a b", t=8, r=2, a=4) for _ in range(2)]
                KTc = [ar.bf16(1024).rearrange("p (k b) -> p k b", b=128) for _ in range(2)]
                Sc = [ar.f32(512).rearrange("p (a r m) -> p a r m", a=4, r=2) for _ in range(2)]
                T1s = [ar.f32(512).rearrange("p (a r m) -> p a r m", a=4, r=2) for _ in range(2)]
                T2s = [ar.f32(512).rearrange("p (a r m) -> p a r m", a=4, r=2) for _ in range(2)]
                SBb = [ar.bf16(512).rearrange("p (a r m) -> p a r m", a=4, r=2) for _ in range(2)]
                cz = [ar.f32(4) for _ in range(2)]
                yt = [ar.f32(512) for _ in range(2)]
                mx, sg = yt[0], yt[1]
                wgl = [ar.bf16(1024).rearrange("p (k f) -> p k f", f=128) for _ in range(2)]
                ts = slice(t * TT, (t + 1) * TT)
                if t == 0:
                    norm_mod(0, lambda c: AA[:, l * 16 + c:l * 16 + c + 1], lambda c: mod_ap(l, 0, c), hbuf[0], sq, rs, tmp, perm=True, hkey=("h", 0))
                def ld(c):
                    b = c % 2
                    P.dma("sp", GTc[b], gtd_d[l][c].rearrange("p (t r m) -> p t r m", t=8, r=2), reads=[("GTd", l)], writes=[("GTc", b)])
                    P.dma("sp", WTc[b].rearrange("p t r a b -> p (t r) (a b)"), wtd_d[l].rearrange("p (tr ab) -> p tr ab", tr=16)[:, :, c * 128:(c + 1) * 128], reads=[("WTd", l)], writes=[("WTc", b)])
                    P.dma("sp", KTc[b], ktd_d[l][c].rearrange("p (k b) -> p k b", b=128), reads=[("KTd", l)], writes=[("KTc", b)])

                def sloc(c):
                    b = c % 2
                    S_ = Sc[b]
                    for pl in range(4):
                        rows = slice(32 * pl, 32 * pl + 32)
                        for r in range(2):
                            col = r * 64
                            for tau in range(8):
                                P.op("pe", lambda e: e.matmul(pb[pl][:, col:col + 64], GTc[b][rows, tau, r, :], h[rows, c, tau, :], start=(tau == 0), stop=(tau == 7), tile_position=(32 * pl, 0)),
                                     reads=[("GTc", b), HK], writes=[PB(pl)])
                    for pl in range(4):
                        P.op("act", lambda e: e.copy(S_[:, pl, :, :], pb[pl][:, 0:128].rearrange("p (r m) -> p r m", r=2)), reads=[PB(pl)], writes=[("Sre", b), ("Sim", b)])

                def ks(c):
                    b = c % 2
                    S_ = Sc[b]
                    sb_ = SBb[b]
                    ke = "dve" if b == 0 else "pool"
                    CK = ("CAR", c)
                    SK = [("Sre", b), ("Sim", b)]
                    cr4 = CAR[:, 4 * c:4 * c + 4, 0]
                    ci4 = CAR[:, 4 * c:4 * c + 4, 1]
                    czb = cz[b]
                    P.op(ke, lambda e: e.tensor_copy(sb_[:, :, :, 0], CAR[:, 4 * c:4 * c + 4, :]), reads=[CK], writes=[("SB", b)])
                    if t > 0:
                        l0r = LK[:, 4 * c:4 * c + 4, 0, 0]
                        l0i = LK[:, 4 * c:4 * c + 4, 0, 1]
                        for (dst, a_, b_, op_) in [(S_[:, :, 0, 0], l0r, cr4, ALU.add), (S_[:, :, 0, 0], l0i, ci4, ALU.subtract),
                                                   (S_[:, :, 1, 0], l0r, ci4, ALU.add), (S_[:, :, 1, 0], l0i, cr4, ALU.add)]:
                            P.op(ke, lambda e: e.tensor_tensor(czb, a_, b_, ALU.mult), reads=[CK], writes=[("cz", b)])
                            P.op(ke, lambda e: e.tensor_tensor(dst, dst, czb, op_), reads=[("cz", b)] + SK, writes=SK)
                    T1, T2 = T1s[b], T2s[b]
                    for j in range(6):
                        d = 1 << j
                        n = 64 - d
                        lrb = LK[:, 4 * c:4 * c + 4, j, 0:1].unsqueeze(3).to_broadcast([128, 4, 2, n])
                        lib = LK[:, 4 * c:4 * c + 4, j, 1:2].unsqueeze(3).to_broadcast([128, 4, 2, n])
                        P.op(ke, lambda e: e.tensor_tensor(T1[:, :, :, 0:n], S_[:, :, :, 0:n], lrb, ALU.mult), reads=SK, writes=[("T1", b)])
                        P.op(ke, lambda e: e.tensor_tensor(T2[:, :, :, 0:n], S_[:, :, :, 0:n], lib, ALU.mult), reads=SK, writes=[("T2", b)])
                        P.op(ke, lambda e: e.tensor_tensor(S_[:, :, 0, d:64], S_[:, :, 0, d:64], T1[:, :, 0, 0:n], ALU.add), reads=[("T1", b), ("T2", b)], writes=[("Sre", b)])
                        P.op(ke, lambda e: e.tensor_tensor(S_[:, :, 0, d:64], S_[:, :, 0, d:64], T2[:, :, 1, 0:n], ALU.subtract), reads=[("T2", b)], writes=[("Sre", b)])
                        P.op(ke, lambda e: e.tensor_tensor(S_[:, :, 1, d:64], S_[:, :, 1, d:64], T1[:, :, 1, 0:n], ALU.add), reads=[("T1", b)], writes=[("Sim", b)])
                        P.op(ke, lambda e: e.tensor_tensor(S_[:, :, 1, d:64], S_[:, :, 1, d:64], T2[:, :, 0, 0:n], ALU.add), reads=[("T2", b)], writes=[("Sim", b)])
                    P.op(ke, lambda e: e.tensor_copy(CAR[:, 4 * c:4 * c + 4, :], S_[:, :, :, 63]), reads=SK + [("SB", b)], writes=[CK])
                    P.op("act", lambda e: e.copy(sb_[:, :, :, 1:64], S_[:, :, :, 0:63]), reads=SK, writes=[("SB", b)])

                def yout(c):
                    b = c % 2
                    sb_ = SBb[b]
                    yb = 4 + b
                    for tq_ in range(8):
                        for tau in range(tq_ + 1):
                            P.op("pe", lambda e: e.matmul(pb[yb][:, tq_ * 64:(tq_ + 1) * 64], KTc[b][:, tq_ - tau, :], h[:, c, tau, :], start=(tau == 0), stop=False),
                                 reads=[("KTc", b), HK], writes=[PB(yb)])
                        for pl in range(4):
                            rows = slice(32 * pl, 32 * pl + 32)
                            for r in range(2):
                                P.op("pe", lambda e: e.matmul(pb[yb][rows, tq_ * 64:(tq_ + 1) * 64], WTc[b][:, tq_, r, pl, :], sb_[:, pl, r, :], start=False, stop=(r == 1 and pl == 3), tile_position=(0, 32 * pl)),
                                     reads=[("WTc", b), ("SB", b)], writes=[PB(yb)])
                    ytb = yt[b]
                    P.op("dve", lambda e: e.scalar_tensor_tensor(ytb.rearrange("p (m t) -> p m t", t=8), h[:, c].rearrange("p t m -> p m t"), S5D[:, l * 8 + c:l * 8 + c + 1], pb[yb][:, :].rearrange("p (t m) -> p m t", t=8), ALU.mult, ALU.add), reads=[HK, PB(yb)], writes=[("yt", b)])
                    P.op("act", lambda e: e.activation(gT[:, c, :], ytb, AF.Gelu_apprx_tanh), reads=[("yt", b)], writes=["gT"])

                ld(0)
                sloc(0)
                for c in range(8):
                    if c + 1 < 8:
                        ld(c + 1)
                        sloc(c + 1)
                    ks(c)
                    yout(c)
                    if c == 4 and t + 1 < NT:
                        norm_mod(t + 1, lambda c_: AA[:, l * 16 + c_:l * 16 + c_ + 1], lambda c_: mod_ap(l, 0, c_), hbuf[(t + 1) % 2], sq, rs, tmp, perm=True, hkey=("h", (t + 1) % 2))
                wv = wglu_d[l].rearrange("(k p) f -> p k f", p=128)
                for fc in range(8):
                    b = fc % 2
                    P.dma("pool", wgl[b], wv[:, :, fc * 128:(fc + 1) * 128], writes=[("wgl", b)])
                    for k in range(8):
                        P.op("pe", lambda e, b=b, k=k: e.matmul(pb[6][:, :], wgl[b][:, k, :], gT[:, k, :], start=(k == 0), stop=(k == 7)), reads=[("wgl", b), "gT"], writes=[PB(6)])
                    P.op("act", lambda e, fc=fc: e.activation(sg, pb[6][:, :], AF.Sigmoid, bias=BGL[:, l * 8 + fc:l * 8 + fc + 1], scale=1.0), reads=[PB(6)], writes=[("yt", 1)])
                    P.op("dve", lambda e, fc=fc: e.tensor_tensor(mx, gT[:, fc, :], sg, ALU.mult), reads=["gT", ("yt", 1)], writes=[("yt", 0)])
                    P.op("dve", lambda e, fc=fc: e.scalar_tensor_tensor(xT[:, fc, ts], mx, mod_ap(l, 2, fc), xT[:, fc, ts], ALU.mult, ALU.add), reads=[("yt", 0), ("x", t, fc)], writes=[("x", t, fc)])
            P.barrier()
            if l + 1 in ffn_layers:
                cast_ffn(l + 1)
            if l in ffn_layers:
                for t in range(NT):
                    ffn(l, t, base)
                P.barrier()

        def kv_build(ckv, KTt, base):
            for t in range(NT):
                ar = Arena(AR, base)
                h = ar.bf16(4096).rearrange("p (k n) -> p k n", n=512)
                sq = [ar.bf16(512) for _ in range(2)]
                rs = ar.f32(512)
                rs2 = ar.f32(512)
                tmp = [ar.f32(512) for _ in range(2)]
                tab = ar.f32(1024).rearrange("p (r n) -> p r n", n=512)
                posi = ar.i32(512)
                angs = ar.f32(512)
                t1 = ar.f32(512)
                ti = ar.i32(512)
                wka = ar.bf16(8 * 288).rearrange("p (k f) -> p k f", f=288)
                wkr = ar.bf16(8 * 32).rearrange("p (k f) -> p k f", f=32)
                ts = slice(t * TT, (t + 1) * TT)
                R = slice(64, 96)
                if t == 0:
                    P.dma("pool", wka, wkva_d.rearrange("(k p) f -> p k f", p=128), writes=["wka"])
                    P.dma("pool", wkr, wkvar_d.rearrange("(k p) f -> p k f", p=128), writes=["wkr"])
                norm_mod(t, lambda c: AK[:, c:c + 1], lambda c: KVM[:, c:c + 1], h, sq, rs, tmp)
                for cc in range(2):
                    for k in range(8):
                        P.op("pe", lambda e, cc=cc, k=k: e.matmul(pb[cc][:, :], wka[:, k, cc * 128:(cc + 1) * 128], h[:, k, :], start=(k == 0), stop=(k == 7)), reads=["wka", "h"], writes=[PB(cc)])
                for k in range(8):
                    P.op("pe", lambda e, k=k: e.matmul(pb[2][R, :], wka[:, k, 256:288], h[:, k, :], start=(k == 0), stop=(k == 7), tile_position=(0, 64)), reads=["wka", "h"], writes=[PB(2)])
                for k in range(8):
                    P.op("pe", lambda e, k=k: e.matmul(pb[3][R, :], wkr[:, k, :], h[:, k, :], start=(k == 0), stop=(k == 7), tile_position=(0, 64)), reads=["wkr", "h"], writes=[PB(3)])
                for cc in range(2):
                    P.op("act", lambda e, cc=cc: e.activation(sq[cc], pb[cc][:, :], AF.Square), reads=[PB(cc)], writes=[("sq", cc)])
                    P.op("pe", lambda e, cc=cc: e.matmul(pb[4][:, :], ones256[:], sq[cc], start=(cc == 0), stop=(cc == 1)), reads=[("sq", cc), "ones256"], writes=[PB(4)])
                rstd_from_psum(4, rs2, key="rs2")
                for cc in range(2):
                    P.op("dve", lambda e, cc=cc: e.scalar_tensor_tensor(ckv[:, cc, ts], pb[cc][:, :], KVAG[:, cc:cc + 1], rs2, ALU.mult, ALU.mult), reads=[PB(cc), "rs2"], writes=["ckv"])
                P.dma("sp", posi[R, :], pos_d[64:96, ts], writes=["posi"])
                P.op("dve", lambda e: e.tensor_copy(angs[R, :], posi[R, :]), reads=["posi"], writes=["rsins"])
                P.op("dve", lambda e: e.tensor_scalar_mul(angs[R, :], angs[R, :], CST[R, 0:1]), reads=["rsins", "CST"], writes=["rsins"])
                rsin(tab[R, 1, :], angs[R, :], 512, t1, ti, prange=R)
                P.op("dve", lambda e: e.tensor_scalar_mul(tab[R, 1, :], tab[R, 1, :], CST[R, 1:2]), reads=["rsind"], writes=["tabs"])
                P.op("dve", lambda e: e.tensor_scalar_add(angs[R, :], angs[R, :], 0.5 * math.pi), reads=["rsins", "rsint"], writes=["rsins"])
                rsin(tab[R, 0, :], angs[R, :], 512, t1, ti, prange=R)
                P.dma("sp", tab_d[64:96, :, ts], tab[R, :, :], reads=["rsind", "tabs"], writes=["tabd"])
                P.op("act", lambda e: e.activation(sq[0][R, :], pb[2][R, :], AF.Square), reads=[PB(2)], writes=[("sq", 0)])
                P.op("pe", lambda e: e.matmul(pb[5][R, :], bq[R, 64:96], sq[0][R, :], start=True, stop=True, tile_position=(64, 64)), reads=[("sq", 0), "bq"], writes=[PB(5)])
                rstd_from_psum(5, rs, prange=R, key="rsr")
                P.op("dve", lambda e: e.scalar_tensor_tensor(tmp[0][R, :], pb[2][R, :], KG[R, 0:1], tab[R, 0, :], ALU.mult, ALU.mult), reads=[PB(2), "rsind"], writes=[("tmp", 0)])
                P.op("dve", lambda e: e.scalar_tensor_tensor(tmp[1][R, :], pb[3][R, :], KG[R, 1:2], tab[R, 1, :], ALU.mult, ALU.mult), reads=[PB(3), "tabs"], writes=[("tmp", 1)])
                P.op("dve", lambda e: e.tensor_tensor(tmp[0][R, :], tmp[0][R, :], tmp[1][R, :], ALU.add), reads=[("tmp", 0), ("tmp", 1)], writes=[("tmp", 0)])
                P.op("dve", lambda e: e.tensor_tensor(KTt[R, ts], tmp[0][R, :], rs[R, :], ALU.mult), reads=[("tmp", 0), "rsr"], writes=["KTr"])
                P.barrier()

        def mla_layer(l, ckv, KTt, base):
            jl = l - 2
            if mla_stage < 1:
                return
            ar = Arena(AR, base)
            qn = ar.bf16(8192).rearrange("p (c n) -> p c n", n=S)
            loc = ar.off
            wdq = ar.bf16(8 * 256).rearrange("p (k f) -> p k f", f=256)
            h = ar.bf16(4096).rearrange("p (k n) -> p k n", n=512)
            sq = [ar.bf16(512) for _ in range(2)]
            rs = ar.f32(512)
            rs2 = ar.f32(512)
            tmp = [ar.f32(512) for _ in range(2)]
            P.dma("pool", wdq, wdq_d[jl].rearrange("(k p) f -> p k f", p=128), writes=["wdq"])
            for t in range(NT):
                ts = slice(t * TT, (t + 1) * TT)
                norm_mod(t, lambda c: AA[:, l * 16 + c:l * 16 + c + 1], lambda c: mod_ap(l, 0, c), h, sq, rs, tmp)
                for cc in range(2):
                    for k in range(8):
                        P.op("pe", lambda e, cc=cc, k=k: e.matmul(pb[cc][:, :], wdq[:, k, cc * 128:(cc + 1) * 128], h[:, k, :], start=(k == 0), stop=(k == 7)), reads=["wdq", "h"], writes=[PB(cc)])
                for cc in range(2):
                    P.op("act", lambda e, cc=cc: e.activation(sq[cc], pb[cc][:, :], AF.Square), reads=[PB(cc)], writes=[("sq", cc)])
                    P.op("pe", lambda e, cc=cc: e.matmul(pb[4][:, :], ones256[:], sq[cc], start=(cc == 0), stop=(cc == 1)), reads=[("sq", cc), "ones256"], writes=[PB(4)])
                rstd_from_psum(4, rs2, key="rs2")
                for cc in range(2):
                    P.op("dve", lambda e, cc=cc, ts=ts: e.scalar_tensor_tensor(qn[:, cc, ts], pb[cc][:, :], QNG[:, jl * 2 + cc:jl * 2 + cc + 1], rs2, ALU.mult, ALU.mult), reads=[PB(cc), "rs2"], writes=["qn"])
            P.barrier()
            if mla_stage < 2:
                return
            ar = Arena(AR, loc)
            Vaug = ar.bf16(4096).rearrange("p (t d) -> p t d", d=128)
            OT = ar.bf16(4096)
            PT = [ar.bf16(512) for _ in range(3)]
            QT = [ar.bf16(512) for _ in range(2)]
            sqh = ar.bf16(512)
            rsq = ar.f32(512)
            rsk = rsq
            tq0 = ar.f32(512)
            rinv = ar.f32(512)
            tq1 = rinv
            tab = ar.f32(1024).rearrange("p (r n) -> p r n", n=512)
            wuq2 = [ar.bf16(2 * 96).rearrange("p (c f) -> p c f", f=96) for _ in range(2)]
            wuqr2 = [ar.bf16(2 * 32).rearrange("p (c f) -> p c f", f=32) for _ in range(2)]
            wkb2 = [ar.bf16(2 * 128).rearrange("p (c f) -> p c f", f=128) for _ in range(2)]
            wo = ar.bf16(1024)
            P.op("dve", lambda e: e.memset(Vaug[:, :, 64:128], 1.0), writes=["Vaug"])
            R = slice(64, 96)
            N_ = slice(0, 64)
            for hp in range(n_hp):
                for hh in range(2):
                    hd = 2 * hp + hh
                    P.dma("pool", wuq2[hh], wuq_d[jl].rearrange("(c p) f -> p c f", p=128)[:, :, hd * 96:(hd + 1) * 96], writes=[("wuq", hh)])
                    P.dma("pool", wuqr2[hh], wuqr_d[jl].rearrange("(c p) f -> p c f", p=128)[:, :, hd * 32:(hd + 1) * 32], writes=[("wuqr", hh)])
                    P.dma("pool", wkb2[hh], wkvb_d.rearrange("(c p) f -> p c f", p=128)[:, :, hd * 128:(hd + 1) * 128], writes=[("wkb", hh)])
                P.dma("pool", wo, wo_d[jl][hp * 128:(hp + 1) * 128, :], writes=["wo"])
                for hh in range(2):
                    hd = 2 * hp + hh
                    wuq, wuqr, wkb = wuq2[hh], wuqr2[hh], wkb2[hh]
                    WQ, WR, WK = ("wuq", hh), ("wuqr", hh), ("wkb", hh)
                    rsb = [(rsq, "rsq"), (tq0, "tq0")]
                    sqb = [(sqh, "sqh"), (PT[0], ("PT", 0))]

                    def vgroup(g):
                        for i in range(8):
                            tt_ = g * 8 + i
                            for cc in range(2):
                                P.op("pe", lambda e: e.matmul(pb[4][:, i * 64:(i + 1) * 64], ckv[:, cc, tt_ * 128:(tt_ + 1) * 128], wkb[:, cc, 64:128], start=(cc == 0), stop=(cc == 1)), reads=[WK, "ckv"], writes=[PB(4)])
                        P.op("act", lambda e: e.copy(Vaug[:, g * 8:(g + 1) * 8, 0:64], pb[4][:, :].rearrange("p (a b) -> p a b", b=64)), reads=[PB(4)], writes=["Vaug"])
                    for t in range(NT):
                        ts = slice(t * TT, (t + 1) * TT)
                        kb = t % 2
                        rsk_, rkey = rsb[kb]
                        sq_, skey = sqb[kb]
                        for cc in range(2):
                            P.op("pe", lambda e: e.matmul(pb[kb][N_, :], wkb[:, cc, 0:64], ckv[:, cc, ts], start=(cc == 0), stop=(cc == 1)), reads=[WK, "ckv"], writes=[PB(kb)])
                        P.op("act", lambda e: e.activation(sq_[N_, :], pb[kb][N_, :], AF.Square), reads=[PB(kb)], writes=[skey])
                        if t % 2 == 1:
                            vgroup(t // 2)
                        P.op("pe", lambda e: e.matmul(pb[2 + kb][N_, :], bq[N_, 0:64], sq_[N_, :], start=True, stop=True), reads=[skey, "bq"], writes=[PB(2 + kb)])
                        rstd_from_psum(2 + kb, rsk_, prange=N_, key=rkey, lntmp=rinv, lnkey="rinv")
                        P.op("dve", lambda e: e.scalar_tensor_tensor(KTt[N_, ts], pb[kb][N_, :], KG[N_, 0:1], rsk_[N_, :], ALU.mult, ALU.mult), reads=[PB(kb), rkey], writes=["KTn"])
                    def qprep(t):
                        ts = slice(t * TT, (t + 1) * TT)
                        qt_ = QT[t % 2]
                        QK_ = ("QT", t % 2)
                        P.dma("sp", tab[R, :, :], tab_d[64:96, :, ts], reads=["tabd"], writes=["tab"])
                        for cc in range(2):
                            P.op("pe", lambda e, cc=cc, ts=ts: e.matmul(pb[5][0:96, :], wuq[:, cc, :], qn[:, cc, ts], start=(cc == 0), stop=(cc == 1)), reads=[WQ, "qn"], writes=[PB(5)])
                        for cc in range(2):
                            P.op("pe", lambda e, cc=cc, ts=ts: e.matmul(pb[6][R, :], wuqr[:, cc, :], qn[:, cc, ts], start=(cc == 0), stop=(cc == 1), tile_position=(0, 64)), reads=[WR, "qn"], writes=[PB(6)])
                        P.op("act", lambda e: e.activation(sqh[0:96, :], pb[5][0:96, :], AF.Square), reads=[PB(5)], writes=["sqh"])
                        P.op("pe", lambda e: e.matmul(pb[7][0:96, :], bq[0:96, 0:96], sqh[0:96, :], start=True, stop=True), reads=["sqh", "bq"], writes=[PB(7)])
                        rstd_from_psum(7, rsq, prange=slice(0, 96), key="rsq", lntmp=tq0, lnkey="tq0")
                        P.op("dve", lambda e: e.scalar_tensor_tensor(qt_[N_, :], pb[5][N_, :], QG[N_, jl * 2:jl * 2 + 1], rsq[N_, :], ALU.mult, ALU.mult), reads=[PB(5), "rsq"], writes=[QK_])
                        P.op("dve", lambda e: e.scalar_tensor_tensor(tq0[R, :], pb[5][R, :], QG[R, jl * 2:jl * 2 + 1], tab[R, 0, :], ALU.mult, ALU.mult), reads=[PB(5), "tab"], writes=["tq0"])
                        P.op("dve", lambda e: e.scalar_tensor_tensor(tq1[R, :], pb[6][R, :], QG[R, jl * 2 + 1:jl * 2 + 2], tab[R, 1, :], ALU.mult, ALU.mult), reads=[PB(6), "tab"], writes=["rinv"])
                        P.op("dve", lambda e: e.tensor_tensor(tq0[R, :], tq0[R, :], tq1[R, :], ALU.add), reads=["tq0", "rinv"], writes=["tq0"])
                        P.op("dve", lambda e: e.tensor_tensor(qt_[R, :], tq0[R, :], rsq[R, :], ALU.mult), reads=["tq0", "rsq"], writes=[QK_])

                    if n_qt > 0:
                        qprep(0)
                    for t in range(n_qt):
                        ts = slice(t * TT, (t + 1) * TT)
                        if t + 1 < n_qt:
                            qprep(t + 1)
                        qt_ = QT[t % 2]
                        QK_ = ("QT", t % 2)
                        nk = 4 * t + 4
                        ob = 2 + t % 2

                        def qk(kt):
                            jd = kt - 4 * t
                            c0 = 128 * jd if jd > 0 else 0
                            sbk = (0, 1, 4)[kt % 3]
                            pt = PT[kt % 3]
                            P.op("pe", lambda e: e.matmul(pb[sbk][:, c0:512], KTt[0:96, kt * 128:(kt + 1) * 128], qt_[0:96, c0:512], start=True, stop=True), reads=["KTn", "KTr", QK_], writes=[PB(sbk)])
                            P.op("act", lambda e: e.activation(pt[:, c0:512], pb[sbk][:, c0:512], AF.Exp, scale=ATTN_SCALE), reads=[PB(sbk)], writes=[("PT", kt % 3)])
                            if jd >= 0:
                                P.op("pool", lambda e: e.memset(pt[64:128, c0:c0 + 64], 0.0), reads=[], writes=[("PT", kt % 3)])

                        def pv(kt):
                            jd = kt - 4 * t
                            c0 = 128 * jd if jd > 0 else 0
                            pt = PT[kt % 3]
                            P.op("pe", lambda e: e.matmul(pb[ob][:, c0:512], Vaug[:, kt, :], pt[:, c0:512], start=(kt == 0), stop=(kt == nk - 1)), reads=["Vaug", ("PT", kt % 3)], writes=[PB(ob)])
                        for kt in range(nk):
                            qk(kt)
                            if kt >= 1:
                                pv(kt - 1)
                        pv(nk - 1)
                        P.op("dve", lambda e: e.reciprocal(rinv[64:128, :], pb[ob][64:128, :]), reads=[PB(ob)], writes=["rinv"])
                        P.op("dve", lambda e: e.tensor_copy(rinv[0:64, :], rinv[64:128, :]), reads=["rinv"], writes=["rinv"])
                        P.op("dve", lambda e: e.tensor_tensor(OT[64 * hh:64 * hh + 64, ts], pb[ob][0:64, :], rinv[0:64, :], ALU.mult), reads=[PB(ob), "rinv"], writes=["OT"])
                for t in range(NT):
                    ts = slice(t * TT, (t + 1) * TT)
                    for dc in range(8):
                        ob = (4, 5, 0, 1)[dc % 4]
                        P.op("pe", lambda e, dc=dc, ob=ob, ts=ts: e.matmul(pb[ob][:, :], wo[:, dc * 128:(dc + 1) * 128], OT[:, ts], start=True, stop=True), reads=["wo", "OT"], writes=[PB(ob)])
                        if dc not in (1, 4, 6):
                            P.op("dve", lambda e, dc=dc, ob=ob, ts=ts: e.scalar_tensor_tensor(xT[:, dc, ts], pb[ob][:, :], mod_ap(l, 2, dc), xT[:, dc, ts], ALU.mult, ALU.add), reads=[PB(ob), ("x", t, dc)], writes=[("x", t, dc)])
                        else:
                            tb_, tk_ = [(tq0, "tq0"), (rinv, "rinv")][dc % 2]
                            P.op("act", lambda e, dc=dc, ob=ob, tb_=tb_: e.activation(tb_, pb[ob][:, :], AF.Copy, scale=mod_ap(l, 2, dc)), reads=[PB(ob)], writes=[tk_])
                            P.op("pool", lambda e, dc=dc, ts=ts, tb_=tb_: e.tensor_tensor(xT[:, dc, ts], xT[:, dc, ts], tb_, ALU.add), reads=[tk_, ("x", t, dc)], writes=[("x", t, dc)])
            P.barrier()
            if l == 2 and 3 in ffn_layers:
                cast_ffn(3)
            if l in ffn_layers:
                for t in range(NT):
                    ffn(l, t, base)
                P.barrier()

        for l in range(min(2, n_layers)):
            if s5_on:
                s5_layer(l)
        if n_layers > 2:
            ar = Arena(AR)
            ckv = ar.bf16(8192).rearrange("p (c n) -> p c n", n=S)
            KTt = ar.bf16(4096)
            base2 = ar.off
            P.barrier()
            kv_build(ckv, KTt, base2)
            for l in range(2, n_layers):
                mla_layer(l, ckv, KTt, base2)

        P.barrier()
        ar = Arena(AR, 6200)
        stg = [ar.f32(1024) for _ in range(2)]
        for t in range(32):
            b = t % 2
            for hf in range(2):
                bk = 2 + b * 2 + hf
                for j in range(4):
                    c = hf * 4 + j
                    P.op("pe", lambda e, bk=bk, j=j, c=c, t=t: e.matmul(pb[bk][:, j * 128:(j + 1) * 128], xT[:, c, t * 128:(t + 1) * 128], ident[:], start=True, stop=True, is_transpose=True),
                         reads=[("x", t // 4, c), "ident"], writes=[PB(bk)])
                if hf == 0:
                    P.op("dve", lambda e, bk=bk, b=b: e.tensor_copy(stg[b][:, 0:512], pb[bk][:, :]), reads=[PB(bk)], writes=[("stg", b)])
                else:
                    P.op("act", lambda e, bk=bk, b=b: e.copy(stg[b][:, 512:1024], pb[bk][:, :]), reads=[PB(bk)], writes=[("stg", b)])
            P.dma("sp", out_d[t * 128:(t + 1) * 128, :], stg[b], reads=[("stg", b)], is_output=True)
        P.finish()
        build.last_counts = dict(P.cnt)
        build.sbuf_left = nc.sbuf_bytes_remaining
    return nc


def _fm(v, nchunk):
    return np.ascontiguousarray(np.asarray(v, np.float32).reshape(nchunk, 128).T)


def _host_layout(inp):
    f = lambda a: np.ascontiguousarray(np.asarray(a, dtype=np.float32))
    sh = {}
    sh["ada_w"] = f(inp["ada_w"])
    sh["ada_b"] = np.ascontiguousarray(np.concatenate([_fm(inp["ada_b"][l], 48) for l in range(4)], axis=1))
    g12 = []
    for l in range(4):
        g12 += [_fm(inp["norm1_g"][l], 8), _fm(inp["norm2_g"][l], 8)]
    sh["g12"] = np.ascontiguousarray(np.concatenate(g12, axis=1))
    sh["wg"] = f(inp["ffn_w_gate"]); sh["wu"] = f(inp["ffn_w_up"]); sh["wd"] = f(inp["ffn_w_down"])
    s5p = np.zeros((2, 128, 96), np.float32)
    s5b = np.zeros((2, 128, 2, 32, 32), np.float32)
    s5c = np.zeros((2, 128, 2, 32, 32), np.float32)
    for l in range(2):
        lre = np.asarray(inp["s5_lam_re"][l]).reshape(32, 2, 64)
        lim = np.asarray(inp["s5_lam_im"][l]).reshape(32, 2, 64)
        ldt = np.broadcast_to(np.asarray(inp["s5_log_dt"][l]).reshape(32, 2, 1), (32, 2, 64))
        s5p[l, :, 0:32] = lre.transpose(1, 2, 0).reshape(128, 32)
        s5p[l, :, 32:64] = lim.transpose(1, 2, 0).reshape(128, 32)
        s5p[l, :, 64:96] = ldt.transpose(1, 2, 0).reshape(128, 32)
        for r, (bk, ck) in enumerate([("s5_b_re", "s5_c_re"), ("s5_b_im", "s5_c_im")]):
            b = np.asarray(inp[bk][l]).reshape(32, 2, 64, 16)
            c = np.asarray(inp[ck][l]).reshape(32, 2, 16, 64)
            for g2 in range(2):
                s5b[l, g2 * 64:(g2 + 1) * 64, r, :, g2 * 16:(g2 + 1) * 16] = b[:, g2].transpose(1, 0, 2)
                s5c[l, g2 * 64:(g2 + 1) * 64, r, :, g2 * 16:(g2 + 1) * 16] = c[:, g2].transpose(2, 0, 1)
    sh["s5p"] = s5p; sh["s5b"] = s5b.reshape(2, 128, 2048); sh["s5c"] = s5c.reshape(2, 128, 2048)
    sh["s5d"] = np.ascontiguousarray(np.concatenate([_fm(inp["s5_d"][l], 8) for l in range(2)], axis=1))
    sh["wglu"] = f(inp["s5_w_glu"])
    sh["bglu"] = np.ascontiguousarray(np.concatenate([_fm(inp["s5_b_glu"][l], 8) for l in range(2)], axis=1))
    sh["kvaw"] = f(inp["kv_ada_w"]); sh["kvab"] = _fm(inp["kv_ada_b"], 16); sh["kvg"] = _fm(inp["kv_norm_g"], 8)
    wkva = f(inp["w_kv_a"])
    sh["wkva"] = wkva
    perm = (np.arange(32) + 16) % 32
    sh["wkvar"] = np.ascontiguousarray(wkva[:, 256 + perm])
    sh["kvag"] = _fm(inp["kv_a_norm_g"], 2)
    sh["wkvb"] = f(inp["w_kv_b"])
    kg = np.zeros((96, 2), np.float32)
    kg[0:64, 0] = np.asarray(inp["k_nope_norm_g"]); kg[64:96, 0] = np.asarray(inp["k_rope_norm_g"]); kg[64:96, 1] = np.asarray(inp["k_rope_norm_g"])[perm]
    sh["kg"] = kg
    sh["wdq"] = f(inp["mla_w_dq"])
    sh["qng"] = np.ascontiguousarray(np.concatenate([_fm(inp["mla_q_norm_g"][j], 2) for j in range(2)], axis=1))
    wuq = f(inp["mla_w_uq"])
    sh["wuq"] = wuq
    cols = np.concatenate([h * 96 + 64 + perm for h in range(16)])
    sh["wuqr"] = np.ascontiguousarray(wuq[:, :, cols])
    qg = np.zeros((96, 4), np.float32)
    for j in range(2):
        qg[0:64, 2 * j] = np.asarray(inp["mla_q_nope_norm_g"][j]); qg[64:96, 2 * j] = np.asarray(inp["mla_q_rope_norm_g"][j])
        qg[64:96, 2 * j + 1] = np.asarray(inp["mla_q_rope_norm_g"][j])[perm]
    sh["qg"] = qg
    sh["wo"] = f(inp["mla_w_o"])
    sh["ident"] = np.eye(128, dtype=np.float32)
    bq = np.zeros((96, 96), np.float32); bq[0:64, 0:64] = 1.0 / 64.0; bq[64:96, 64:96] = 1.0 / 32.0
    sh["bq"] = bq
    cst = np.zeros((96, 2), np.float32)
    k = np.arange(32) % 16
    cst[64:96, 0] = (1.0 / (10000.0 ** (np.arange(0, 32, 2, dtype=np.float32) / 32.0)))[k]
    cst[64:96, 1] = np.where(np.arange(32) < 16, -1.0, 1.0)
    sh["cst"] = cst
    return sh


_NC_CACHE = {}


def kernel(**inputs):
    x = np.asarray(inputs["x"], np.float32)
    c = np.asarray(inputs["c"], np.float32)
    pos = np.asarray(inputs["positions"], np.int32)
    sh = _host_layout(inputs)
    if "nc" not in _NC_CACHE:
        _NC_CACHE["nc"] = build()
    nc = _NC_CACHE["nc"]
    in_maps = []
    for b in range(8):
        m = dict(sh)
        m["x"] = np.ascontiguousarray(x[b])
        m["cT"] = _fm(c[b], 8)
        m["pos"] = np.ascontiguousarray(np.broadcast_to(pos[b][None, :], (96, S))).astype(np.int32)
        in_maps.append(m)
    res = run_bass_kernel_spmd(nc, in_maps, core_ids=list(range(8)))
    return np.stack([np.asarray(r["out"], np.float32) for r in res.results], axis=0)
```

```python
import numpy as np
import concourse.bass as bass
import concourse.mybir as mybir
from concourse.bass_utils import run_bass_kernel_spmd
from contextlib import ExitStack
import types


def _freeze(fn):
    if fn.__closure__ is None:
        return fn
    cells = []
    for c in fn.__closure__:
        try:
            cells.append(types.CellType(c.cell_contents))
        except ValueError:
            cells.append(c)
    return types.FunctionType(fn.__code__, fn.__globals__, fn.__name__, fn.__defaults__, tuple(cells))

F32 = mybir.dt.float32
BF16 = mybir.dt.bfloat16
I32 = mybir.dt.int32
AF = mybir.ActivationFunctionType
ALU = mybir.AluOpType
AX = mybir.AxisListType


class Prog:
    CE = ("pe", "act", "dve", "pool")
    NPOOL = 20

    def __init__(self, nc, es):
        self.nc = nc
        self.es = es
        self.q = {e: [] for e in self.CE + ("sp",)}
        self.sem = {e: es.enter_context(nc.semaphore("s_" + e)) for e in self.CE}
        self.cnt = {e: 0 for e in self.CE}
        self.seen = {e: {} for e in self.CE + ("sp",)}
        self.bufs = {}
        self.dpool = {}
        for qn in ("sp", "pool", "act"):
            self.dpool[qn] = [[es.enter_context(nc.semaphore("d_%s%d" % (qn, i))), 0]
                              for i in range(self.NPOOL)]
        self.dnext = {"sp": 0, "pool": 0, "act": 0}
        self.semobj = {}
        for e in self.CE:
            self.semobj[("e", e)] = self.sem[e]
        for qn in self.dpool:
            for i, (s, _) in enumerate(self.dpool[qn]):
                self.semobj[("d", qn, i)] = s
        self.out_waits = []
        self.n_inst = 0

    def _need(self, eng, reads, writes):
        need = {}

        def add(t):
            k, v = t
            if need.get(k, 0) < v:
                need[k] = v
        for k in reads:
            b = self.bufs.get(k)
            if b is not None and b["w"] is not None:
                add(b["w"])
        for k in writes:
            b = self.bufs.get(k)
            if b is not None:
                if b["w"] is not None:
                    add(b["w"])
                for t in b["r"].items():
                    add(t)
        return need

    def _emit_waits(self, eng, need):
        for k, v in need.items():
            if eng == "pe" and k == ("e", "pe"):
                continue
            if self.seen[eng].get(k, 0) >= v:
                continue
            self.seen[eng][k] = v
            s = self.semobj[k]
            self.q[eng].append(lambda e, s=s, v=v: e.wait_ge(s, v))
            self.n_inst += 1

    def _record(self, tag, reads, writes):
        for k in reads:
            b = self.bufs.setdefault(k, {"w": None, "r": {}})
            if b["r"].get(tag[0], 0) < tag[1]:
                b["r"][tag[0]] = tag[1]
        for k in writes:
            self.bufs[k] = {"w": tag, "r": {}}

    def op(self, eng, fn, reads=(), writes=()):
        fn = _freeze(fn)
        need = self._need(eng, reads, writes)
        self._emit_waits(eng, need)
        self.cnt[eng] += 1
        c = self.cnt[eng]
        s = self.sem[eng]
        self.q[eng].append(lambda e, fn=fn, s=s: fn(e).then_inc(s, 1))
        self.n_inst += 1
        self._record((("e", eng), c), reads, writes)
        return c

    def dma(self, qn, out, in_, reads=(), writes=(), is_output=False, **kw):
        need = self._need(qn, reads, writes)
        i = self.dnext[qn]
        self.dnext[qn] = (i + 1) % self.NPOOL
        ent = self.dpool[qn][i]
        key = ("d", qn, i)
        if ent[1] > 0:
            if need.get(key, 0) < ent[1]:
                need[key] = ent[1]
        self._emit_waits(qn, need)
        ent[1] += 16
        v = ent[1]
        s = ent[0]
        self.q[qn].append(lambda e, s=s, out=out, in_=in_, kw=kw: e.dma_start(out=out, in_=in_, **kw).then_inc(s, 16))
        self.n_inst += 1
        self._record((key, v), reads, writes)
        if is_output:
            self.out_waits.append((key, v))

    def barrier(self):
        for e in self.CE + ("sp",):
            need = {("e", f): self.cnt[f] for f in self.CE if self.cnt[f] > 0}
            for qn in self.dpool:
                for i, ent in enumerate(self.dpool[qn]):
                    if ent[1] > 0:
                        need[("d", qn, i)] = ent[1]
            if e == "pe":
                need.pop(("e", "pe"), None)
            self._emit_waits(e, need)

    def finish(self):
        need = {}
        for k, v in self.out_waits:
            if need.get(k, 0) < v:
                need[k] = v
        self._emit_waits("sp", need)
        self.barrier()
        nc = self.nc
        with nc.Block() as block:
            @block.tensor
            def _(e):
                for f in self.q["pe"]:
                    f(e)

            @block.scalar
            def _(e):
                for f in self.q["act"]:
                    f(e)

            @block.vector
            def _(e):
                for f in self.q["dve"]:
                    f(e)

            @block.gpsimd
            def _(e):
                for f in self.q["pool"]:
                    f(e)

            @block.sync
            def _(e):
                for f in self.q["sp"]:
                    f(e)

import math

S = 4096
NT = 8
TT = 512
EPS = 1e-6
TWO_PI = 2.0 * math.pi
ATTN_SCALE = 1.0 / math.sqrt(96.0)
ARENA_F = 19480


class Arena:
    def __init__(self, ap, off=0):
        self.ap = ap
        self.off = off

    def f32(self, n):
        v = self.ap[:, self.off:self.off + n]
        self.off += n
        assert self.off <= ARENA_F, self.off
        return v

    def bf16(self, n):
        m = (n + 1) // 2
        v = self.ap[:, self.off:self.off + m].bitcast(BF16)
        self.off += m
        assert self.off <= ARENA_F, self.off
        return v[:, 0:n]

    def i32(self, n):
        v = self.ap[:, self.off:self.off + n].bitcast(I32)
        self.off += n
        assert self.off <= ARENA_F, self.off
        return v


def build(n_layers=4, ffn_layers=(0, 1, 2, 3), mix_layers=(0, 1, 2, 3), prologue=True, s5_on=True, mla_stage=3, n_hp=8, n_qt=NT):
    nc = bass.Bass("TRN2", target_bir_lowering=False)

    def D(name, shape, dt=F32, kind="ExternalInput"):
        return nc.dram_tensor(name, list(shape), dt, kind=kind).ap()

    x_d = D("x", [S, 1024])
    out_d = D("out", [S, 1024], kind="ExternalOutput")
    cT_d = D("cT", [128, 8])
    ada_w = D("ada_w", [4, 1024, 6144])
    ada_b = D("ada_b", [128, 4 * 48])
    g12_d = D("g12", [128, 4 * 2 * 8])
    wg_d = D("wg", [4, 1024, 2816])
    wu_d = D("wu", [4, 1024, 2816])
    wd_d = D("wd", [4, 2816, 1024])
    s5p_d = D("s5p", [2, 128, 96])
    s5b_d = D("s5b", [2, 128, 2048])
    s5c_d = D("s5c", [2, 128, 2048])
    s5d_d = D("s5d", [128, 16])
    wglu_d = D("wglu", [2, 1024, 1024])
    bglu_d = D("bglu", [128, 16])
    kvaw_d = D("kvaw", [1024, 2048])
    kvab_d = D("kvab", [128, 16])
    kvg_d = D("kvg", [128, 8])
    wkva_d = D("wkva", [1024, 288])
    wkvar_d = D("wkvar", [1024, 32])
    kvag_d = D("kvag", [128, 2])
    wkvb_d = D("wkvb", [256, 2048])
    kg_d = D("kg", [96, 2])
    wdq_d = D("wdq", [2, 1024, 256])
    qng_d = D("qng", [128, 4])
    wuq_d = D("wuq", [2, 256, 1536])
    wuqr_d = D("wuqr", [2, 256, 512])
    qg_d = D("qg", [96, 4])
    wo_d = D("wo", [2, 1024, 1024])
    ident_d = D("ident", [128, 128])
    bq_d = D("bq", [96, 96])
    cst_d = D("cst", [96, 2])
    pos_d = D("pos", [96, S], I32)
    tab_d = D("ropetab", [96, 2, S], kind="Internal")
    wgs_d = D("wgs", [4, 22, 128, 1024], BF16, kind="Internal")
    wus_d = D("wus", [4, 22, 128, 1024], BF16, kind="Internal")
    wds_d = D("wds", [4, 16, 128, 1408], BF16, kind="Internal")
    gtd_d = D("s5gt", [2, 8, 128, 2048], BF16, kind="Internal")
    wtd_d = D("s5wt", [2, 128, 16384], BF16, kind="Internal")
    ktd_d = D("s5kt", [2, 8, 128, 1024], BF16, kind="Internal")

    es = ExitStack()
    with es:
        es.enter_context(nc.allow_low_precision("bf16 matmul operands with fp32 PSUM accumulation"))
        P = Prog(nc, es)

        def SB(name, shape, dt=F32):
            return es.enter_context(nc.sbuf_tensor(name, list(shape), dt))

        xT = SB("xT", [128, 8, S])
        AR = SB("arena", [128, ARENA_F])
        pb = [es.enter_context(nc.psum_tensor("pb%d" % i, [128, 512], F32)) for i in range(8)]
        ident = SB("identsb", [128, 128])
        onesD = SB("onesD", [128, 128], BF16)
        ones256 = SB("ones256", [128, 128], BF16)
        bq = SB("bqsb", [96, 96], BF16)
        MOD = SB("MOD", [128, 4 * 48])
        KVM = SB("KVM", [128, 16])
        ADB = SB("ADB", [128, 4 * 48])
        G12 = SB("G12", [128, 64])
        AA = SB("AA", [128, 64])
        KVB = SB("KVB", [128, 16])
        KVG = SB("KVG", [128, 8])
        AK = SB("AK", [128, 8])
        S5D = SB("S5D", [128, 16])
        BGL = SB("BGL", [128, 16])
        KVAG = SB("KVAG", [128, 2])
        KG = SB("KG", [96, 2])
        QNG = SB("QNG", [128, 4])
        QG = SB("QG", [96, 4])
        CST = SB("CST", [96, 2])
        cact = SB("cact", [128, 8])
        cab = SB("cab", [128, 8], BF16)
        EPSC = SB("EPSC", [128, 1])

        PB = lambda i: ("pb", i)

        def mod_ap(l, kind, c):
            j = l * 48 + kind * 8 + c
            return MOD[:, j:j + 1]

        P.op("dve", lambda e: e.memset(onesD[:], 1.0 / 1024.0), writes=["onesD"])
        P.op("dve", lambda e: e.memset(ones256[:], 1.0 / 256.0), writes=["ones256"])
        P.op("dve", lambda e: e.memset(EPSC[:], EPS), writes=["EPSC"])
        for (dst, src, key) in [(ident, ident_d, "ident"), (ADB, ada_b, "ADB"), (G12, g12_d, "G12"), (KVB, kvab_d, "KVB"),
                                (KVG, kvg_d, "KVG"), (S5D, s5d_d, "S5D"), (BGL, bglu_d, "BGL"), (KVAG, kvag_d, "KVAG"),
                                (KG, kg_d, "KG"), (QNG, qng_d, "QNG"), (QG, qg_d, "QG"), (CST, cst_d, "CST"), (cact, cT_d, "cact")]:
            P.dma("sp", dst[:], src, writes=[key])
        P.dma("pool", bq[:], bq_d, writes=["bq"])
        P.op("act", lambda e: e.activation(cab[:], cact[:], AF.Silu), reads=["cact"], writes=["cab"])

        ar = Arena(AR)
        wbuf = [ar.bf16(4096).rearrange("p (k f) -> p k f", f=512) for _ in range(2)]
        nd = 0
        for l in range(4 if prologue else 0):
            for pc in range(12):
                b = nd % 2
                nd += 1
                P.dma("pool", wbuf[b], ada_w[l].rearrange("(k p) f -> p k f", p=128)[:, :, pc * 512:(pc + 1) * 512], writes=[("wb", b)])
                for j in range(4):
                    col = pc * 4 + j
                    for k in range(8):
                        P.op("pe", lambda e, b=b, j=j, k=k, col=col: e.matmul(pb[1][:, col:col + 1], wbuf[b][:, k, j * 128:(j + 1) * 128], cab[:, k:k + 1], start=(k == 0), stop=(k == 7)),
                             reads=[("wb", b), "cab"], writes=[PB(1)])
            P.op("dve", lambda e, l=l: e.tensor_tensor(MOD[:, l * 48:(l + 1) * 48], pb[1][:, 0:48], ADB[:, l * 48:(l + 1) * 48], ALU.add), reads=[PB(1), "ADB"], writes=["MOD"])
        for pc in range(4 if prologue else 0):
            b = nd % 2
            nd += 1
            P.dma("pool", wbuf[b], kvaw_d.rearrange("(k p) f -> p k f", p=128)[:, :, pc * 512:(pc + 1) * 512], writes=[("wb", b)])
            for j in range(4):
                col = pc * 4 + j
                for k in range(8):
                    P.op("pe", lambda e, b=b, j=j, k=k, col=col: e.matmul(pb[1][:, col:col + 1], wbuf[b][:, k, j * 128:(j + 1) * 128], cab[:, k:k + 1], start=(k == 0), stop=(k == 7)),
                         reads=[("wb", b), "cab"], writes=[PB(1)])
        P.op("dve", lambda e: e.tensor_tensor(KVM[:], pb[1][:, 0:16], KVB[:], ALU.add), reads=[PB(1), "KVB"], writes=["KVM"])
        for l in range(4):
            for n in range(2):
                o = l * 16 + n * 8
                sc = l * 48 + (1 + 3 * n) * 8
                P.op("dve", lambda e, o=o, sc=sc: e.scalar_tensor_tensor(AA[:, o:o + 8], MOD[:, sc:sc + 8], 1.0, G12[:, o:o + 8], ALU.add, ALU.mult),
                     reads=["MOD", "G12"], writes=["AA"])
        P.op("dve", lambda e: e.scalar_tensor_tensor(AK[:], KVM[:, 8:16], 1.0, KVG[:], ALU.add, ALU.mult), reads=["KVM", "KVG"], writes=["AK"])
        P.barrier()

        def cast_ffn(l):
            wgv = wg_d[l].rearrange("(k p) f -> p k f", p=128)
            wuv = wu_d[l].rearrange("(k p) f -> p k f", p=128)
            wdv = wd_d[l].rearrange("(j p) d -> p j d", p=128)
            for hf in range(2):
                for jj in range(11):
                    j = hf * 11 + jj
                    P.dma("pool", wgs_d[l][j].rearrange("p (k f) -> p k f", f=128), wgv[:, :, j * 128:(j + 1) * 128], writes=[("wc", l, "g", j)])
                    P.dma("pool", wus_d[l][j].rearrange("p (k f) -> p k f", f=128), wuv[:, :, j * 128:(j + 1) * 128], writes=[("wc", l, "u", j)])
                for dc in range(8):
                    P.dma("pool", wds_d[l][hf * 8 + dc].rearrange("p (j d) -> p j d", d=128), wdv[:, hf * 11:(hf + 1) * 11, dc * 128:(dc + 1) * 128], writes=[("wc", l, "d", hf * 8 + dc)])

        cast_ffn(0)
        ar = Arena(AR)
        stg = [ar.f32(1024) for _ in range(2)]
        for t in range(32):
            b = t % 2
            P.dma("sp", stg[b], x_d[t * 128:(t + 1) * 128, :], writes=[("stg", b)])
            for hf in range(2):
                bk = 2 + b * 2 + hf
                for j in range(4):
                    c = hf * 4 + j
                    P.op("pe", lambda e, bk=bk, j=j, c=c, b=b: e.matmul(pb[bk][:, j * 128:(j + 1) * 128], stg[b][:, c * 128:(c + 1) * 128], ident[:], start=True, stop=True, is_transpose=True),
                         reads=[("stg", b), "ident"], writes=[PB(bk)])
                eng = "dve" if hf == 0 else "act"
                dst = xT[:, hf * 4:hf * 4 + 4, t * 128:(t + 1) * 128]
                src = pb[bk][:, :].rearrange("p (a b) -> p a b", b=128)
                if eng == "dve":
                    P.op("dve", lambda e, dst=dst, src=src: e.tensor_copy(dst, src), reads=[PB(bk)], writes=[("x", t // 4, hf * 4 + j_) for j_ in range(4)])
                else:
                    P.op("act", lambda e, dst=dst, src=src: e.copy(dst, src), reads=[PB(bk)], writes=[("x", t // 4, hf * 4 + j_) for j_ in range(4)])
        P.barrier()

        def rstd_from_psum(bank, rs, prange=slice(0, 128), key="rs", lntmp=None, lnkey=None):
            if lntmp is not None:
                P.op("act", lambda e: e.activation(lntmp[prange, :], pb[bank][prange, :], AF.Ln, bias=EPSC[prange, 0:1], scale=1.0), reads=[PB(bank), "EPSC"], writes=[lnkey])
                P.op("act", lambda e: e.activation(rs[prange, :], lntmp[prange, :], AF.Exp, scale=-0.5), reads=[lnkey], writes=[key])
                return
            P.op("dve", lambda e: e.tensor_scalar_add(rs[prange, :], pb[bank][prange, :], EPS), reads=[PB(bank)], writes=[key])
            P.op("act", lambda e: e.sqrt(rs[prange, :], rs[prange, :]), reads=[key], writes=[key])
            P.op("dve", lambda e: e.reciprocal(rs[prange, :], rs[prange, :]), reads=[key], writes=[key])

        def norm_mod(t, a_of_c, b_of_c, h, sq, rs, tmp, bank=7, perm=False, hkey="h"):
            ts = slice(t * TT, (t + 1) * TT)
            for c in range(8):
                P.op("act", lambda e, c=c: e.activation(sq[c % 2], xT[:, c, ts], AF.Square), reads=[("x", t, c)], writes=[("sq", c % 2)])
                P.op("pe", lambda e, c=c: e.matmul(pb[bank][:, :], onesD[:], sq[c % 2], start=(c == 0), stop=(c == 7)), reads=[("sq", c % 2), "onesD"], writes=[PB(bank)])
            rstd_from_psum(bank, rs)
            for c in range(8):
                P.op("dve", lambda e, c=c: e.scalar_tensor_tensor(tmp[c % 2], xT[:, c, ts], a_of_c(c), rs, ALU.mult, ALU.mult), reads=[("x", t, c), "rs"], writes=[("tmp", c % 2)])
                if perm:
                    P.op("act", lambda e, c=c: e.activation(h[:, c].rearrange("p t m -> p m t"), tmp[c % 2].rearrange("p (m t) -> p m t", t=8), AF.Identity, bias=b_of_c(c), scale=1.0), reads=[("tmp", c % 2)], writes=[hkey])
                else:
                    P.op("act", lambda e, c=c: e.activation(h[:, c, :], tmp[c % 2], AF.Identity, bias=b_of_c(c), scale=1.0), reads=[("tmp", c % 2)], writes=["h"])

        def ffn(l, t, ar0):
            ar = Arena(AR, ar0)
            h = ar.bf16(4096).rearrange("p (k n) -> p k n", n=512)
            act = ar.bf16(11 * 512).rearrange("p (k n) -> p k n", n=512)
            sq = [ar.bf16(512) for _ in range(2)]
            rs = ar.f32(512)
            tmp = [ar.f32(512) for _ in range(2)]
            sg = [ar.f32(512) for _ in range(2)]
            wgb = [ar.bf16(1024).rearrange("p (k f) -> p k f", f=128) for _ in range(2)]
            wub = [ar.bf16(1024).rearrange("p (k f) -> p k f", f=128) for _ in range(2)]
            wdb = [ar.bf16(11 * 128).rearrange("p (j d) -> p j d", d=128) for _ in range(2)]
            ts = slice(t * TT, (t + 1) * TT)
            norm_mod(t, lambda c: AA[:, l * 16 + 8 + c:l * 16 + 9 + c], lambda c: mod_ap(l, 3, c), h, sq, rs, tmp)
            for hf in range(2):
                for jj in range(11):
                    j = hf * 11 + jj
                    b = jj % 2
                    P.dma("sp", wgb[b].rearrange("p k f -> p (k f)"), wgs_d[l][j], reads=[("wc", l, "g", j)], writes=[("wg", b)])
                    P.dma("sp", wub[b].rearrange("p k f -> p (k f)"), wus_d[l][j], reads=[("wc", l, "u", j)], writes=[("wu", b)])
                    for k in range(8):
                        P.op("pe", lambda e, b=b, k=k: e.matmul(pb[b][:, :], wgb[b][:, k, :], h[:, k, :], start=(k == 0), stop=(k == 7)), reads=[("wg", b), "h"], writes=[PB(b)])
                    for k in range(8):
                        P.op("pe", lambda e, b=b, k=k: e.matmul(pb[2 + b][:, :], wub[b][:, k, :], h[:, k, :], start=(k == 0), stop=(k == 7)), reads=[("wu", b), "h"], writes=[PB(2 + b)])
                    P.op("act", lambda e, b=b: e.activation(sg[b], pb[b][:, :], AF.Silu), reads=[PB(b)], writes=[("sg", b)])
                    P.op("dve", lambda e, b=b, jj=jj: e.tensor_tensor(act[:, jj, :], sg[b], pb[2 + b][:, :], ALU.mult), reads=[("sg", b), PB(2 + b)], writes=[("act", jj)])
                for dc in range(8):
                    b = dc % 2
                    P.dma("sp", wdb[b].rearrange("p j d -> p (j d)"), wds_d[l][hf * 8 + dc], reads=[("wc", l, "d", hf * 8 + dc)], writes=[("wd", b)])
                    for jj in range(11):
                        P.op("pe", lambda e, b=b, jj=jj: e.matmul(pb[4 + b][:, :], wdb[b][:, jj, :], act[:, jj, :], start=(jj == 0), stop=(jj == 10)), reads=[("wd", b), ("act", jj)], writes=[PB(4 + b)])
                    P.op("dve", lambda e, b=b, dc=dc: e.scalar_tensor_tensor(xT[:, dc, ts], pb[4 + b][:, :], mod_ap(l, 5, dc), xT[:, dc, ts], ALU.mult, ALU.add),
                         reads=[PB(4 + b), ("x", t, dc)], writes=[("x", t, dc)])

        def rsin(dst, src, n, t1, ti, prange=slice(0, 128), key="rsin"):
            P.op("dve", lambda e: e.tensor_scalar_mul(t1[prange, 0:n], src, 1.0 / TWO_PI), reads=[key + "s"], writes=[key + "t"])
            P.op("dve", lambda e: e.tensor_copy(ti[prange, 0:n], t1[prange, 0:n]), reads=[key + "t"], writes=[key + "i"])
            P.op("dve", lambda e: e.tensor_copy(t1[prange, 0:n], ti[prange, 0:n]), reads=[key + "i"], writes=[key + "t"])
            P.op("dve", lambda e: e.scalar_tensor_tensor(t1[prange, 0:n], t1[prange, 0:n], -TWO_PI, src, ALU.mult, ALU.add), reads=[key + "t", key + "s"], writes=[key + "t"])
            P.op("act", lambda e: e.activation(dst, t1[prange, 0:n], AF.Sin), reads=[key + "t"], writes=[key + "d"])

        def s5_layer(l):
            ar = Arena(AR)
            LK = ar.f32(32 * 6 * 2).rearrange("p (a j r) -> p a j r", j=6, r=2)
            CAR = ar.f32(64).rearrange("p (a r) -> p a r", r=2)
            base = ar.off
            lam = ar.f32(96)
            P.dma("sp", lam, s5p_d[l], writes=["lam"])
            BB = ar.f32(2048).rearrange("p (r a b) -> p r a b", r=2, b=32)
            CC = ar.f32(2048).rearrange("p (r a b) -> p r a b", r=2, b=32)
            P.dma("sp", BB, s5b_d[l].rearrange("p (r a b) -> p r a b", r=2, b=32), writes=["BB"])
            P.dma("sp", CC, s5c_d[l].rearrange("p (r a b) -> p r a b", r=2, b=32), writes=["CC"])
            FB = ar.f32(2048).rearrange("p (r a b) -> p r a b", r=2, b=32)
            XBf = ar.f32(2048)
            XB = XBf.rearrange("p (r a b) -> p r a b", r=2, b=32)
            TB = ar.f32(1024).rearrange("p (a b) -> p a b", b=32)
            CCn = ar.f32(1024).rearrange("p (a b) -> p a b", b=32)
            PW = ar.f32(32 * 9 * 2).rearrange("p (a k r) -> p a k r", k=9, r=2)
            GTs = ar.bf16(2048).rearrange("p (c r m) -> p c r m", r=2, m=128)
            WTs = ar.bf16(2048).rearrange("p (r a b) -> p r a b", r=2, b=32)
            KTs = ar.bf16(1024).rearrange("p (c b) -> p c b", b=128)
            v = [ar.f32(32) for _ in range(16)]
            ti = ar.i32(32)
            dt, aa, mag, ang, sn, cs, are, aim, nr, den, fre, fim, t0, t1, ang2, t2 = v
            lr, li, ld = lam[:, 0:32], lam[:, 32:64], lam[:, 64:96]
            K = "s5prep"

            def tt(out, a, b, op):
                P.op("dve", lambda e: e.tensor_tensor(out, a, b, op), reads=[K, "lam", "BB", "CC"], writes=[K])
            P.op("act", lambda e: e.activation(dt, ld, AF.Exp), reads=["lam"], writes=[K])
            tt(aa, lr, dt, ALU.mult)
            P.op("act", lambda e: e.activation(mag, aa, AF.Exp), reads=[K], writes=[K])
            tt(ang, li, dt, ALU.mult)
            P.op("dve", lambda e: e.tensor_scalar_add(ang2, ang, 0.5 * math.pi), reads=[K], writes=[K])

            def rs_(dst, src):
                P.op("dve", lambda e: e.tensor_scalar_mul(t1, src, 1.0 / TWO_PI), reads=[K], writes=[K])
                P.op("dve", lambda e: e.tensor_copy(ti, t1), reads=[K], writes=[K])
                P.op("dve", lambda e: e.tensor_copy(t1, ti), reads=[K], writes=[K])
                P.op("dve", lambda e: e.scalar_tensor_tensor(t1, t1, -TWO_PI, src, ALU.mult, ALU.add), reads=[K], writes=[K])
                P.op("act", lambda e: e.activation(dst, t1, AF.Sin), reads=[K], writes=[K])
            rs_(sn, ang)
            rs_(cs, ang2)
            tt(are, mag, cs, ALU.mult)
            tt(aim, mag, sn, ALU.mult)
            P.op("dve", lambda e: e.tensor_scalar_add(nr, are, -1.0), reads=[K], writes=[K])
            tt(den, lr, lr, ALU.mult)
            tt(t0, li, li, ALU.mult)
            tt(den, den, t0, ALU.add)
            P.op("dve", lambda e: e.reciprocal(den, den), reads=[K], writes=[K])
            tt(fre, nr, lr, ALU.mult)
            tt(t0, aim, li, ALU.mult)
            tt(fre, fre, t0, ALU.add)
            tt(fre, fre, den, ALU.mult)
            tt(fim, aim, lr, ALU.mult)
            tt(t0, nr, li, ALU.mult)
            tt(fim, fim, t0, ALU.subtract)
            tt(fim, fim, den, ALU.mult)
            P.op("dve", lambda e: e.memset(PW[:, :, 0, 0], 1.0), reads=[K], writes=[K])
            P.op("dve", lambda e: e.memset(PW[:, :, 0, 1], 0.0), reads=[K], writes=[K])
            for k in range(1, 9):
                pr_, pi_ = PW[:, :, k - 1, 0], PW[:, :, k - 1, 1]
                tt(t0, pr_, are, ALU.mult)
                tt(t2, pi_, aim, ALU.mult)
                tt(PW[:, :, k, 0], t0, t2, ALU.subtract)
                tt(t0, pr_, aim, ALU.mult)
                tt(t2, pi_, are, ALU.mult)
                tt(PW[:, :, k, 1], t0, t2, ALU.add)
            P.op("dve", lambda e: e.tensor_copy(LK[:, :, 0, :], PW[:, :, 8, :]), reads=[K], writes=[K])
            for j in range(1, 6):
                pr_, pi_ = LK[:, :, j - 1, 0], LK[:, :, j - 1, 1]
                tt(t0, pr_, pr_, ALU.mult)
                tt(t2, pi_, pi_, ALU.mult)
                tt(LK[:, :, j, 0], t0, t2, ALU.subtract)
                tt(t0, pr_, pi_, ALU.mult)
                P.op("dve", lambda e, j=j: e.tensor_scalar_mul(LK[:, :, j, 1], t0, 2.0), reads=[K], writes=[K])
            frb = fre.unsqueeze(2).to_broadcast([128, 32, 32])
            fib = fim.unsqueeze(2).to_broadcast([128, 32, 32])
            tt(FB[:, 0], BB[:, 0], frb, ALU.mult)
            tt(TB, BB[:, 1], fib, ALU.mult)
            tt(FB[:, 0], FB[:, 0], TB, ALU.subtract)
            tt(FB[:, 1], BB[:, 1], frb, ALU.mult)
            tt(TB, BB[:, 0], fib, ALU.mult)
            tt(FB[:, 1], FB[:, 1], TB, ALU.add)
            P.op("dve", lambda e: e.tensor_scalar_mul(CCn, CC[:, 1], -1.0), reads=["CC"], writes=[K])
            gtv = gtd_d[l].rearrange("c p (t r m) -> p c t r m", t=8, r=2)
            ktv = ktd_d[l].rearrange("c p (k b) -> p c k b", b=128)
            P.op("dve", lambda e: e.memset(KTs, 0.0), writes=["KTs"])
            for k in range(8):
                prb = PW[:, :, k, 0:1].to_broadcast([128, 32, 32])
                pib = PW[:, :, k, 1:2].to_broadcast([128, 32, 32])
                tt(XB[:, 0], FB[:, 0], prb, ALU.mult)
                tt(TB, FB[:, 1], pib, ALU.mult)
                tt(XB[:, 0], XB[:, 0], TB, ALU.subtract)
                tt(XB[:, 1], FB[:, 1], prb, ALU.mult)
                tt(TB, FB[:, 0], pib, ALU.mult)
                tt(XB[:, 1], XB[:, 1], TB, ALU.add)
                for c in range(8):
                    for r in range(2):
                        bk = (c * 2 + r) % 2
                        P.op("pe", lambda e, c=c, r=r, bk=bk: e.matmul(pb[bk][:, 0:128], XBf[:, r * 1024 + c * 128:r * 1024 + (c + 1) * 128], ident[:], start=True, stop=True, is_transpose=True),
                             reads=[K, "ident"], writes=[PB(bk)])
                        P.op("act", lambda e, c=c, r=r, bk=bk: e.copy(GTs[:, c, r, :], pb[bk][:, 0:128]), reads=[PB(bk)], writes=["GTs"])
                P.dma("sp", gtv[:, :, 7 - k, :, :], GTs, reads=["GTs"], writes=[("GTd", l)])
                for pr in range(32):
                    pl = pr % 4
                    c = pr // 4
                    P.op("pe", lambda e, pr=pr, pl=pl, c=c: e.matmul(pb[2][32 * pl:32 * pl + 32, c * 32:(c + 1) * 32], XB[:, 0, pr, :], CC[:, 0, pr, :], start=True, stop=False, tile_position=(0, 32 * pl)),
                         reads=[K, "CC"], writes=[PB(2)])
                    P.op("pe", lambda e, pr=pr, pl=pl, c=c: e.matmul(pb[2][32 * pl:32 * pl + 32, c * 32:(c + 1) * 32], XB[:, 1, pr, :], CCn[:, pr, :], start=False, stop=True, tile_position=(0, 32 * pl)),
                         reads=[K, "CC"], writes=[PB(2)])
                for pl in range(4):
                    P.op("act", lambda e, pl=pl: e.copy(KTs[32 * pl:32 * pl + 32, :, 32 * pl:32 * pl + 32], pb[2][32 * pl:32 * pl + 32, 0:256].rearrange("p (c b) -> p c b", b=32)), reads=[PB(2)], writes=["KTs"])
                P.dma("sp", ktv[:, :, k, :], KTs, reads=["KTs"], writes=[("KTd", l)])
            for t in range(8):
                prb = PW[:, :, t + 1, 0:1].to_broadcast([128, 32, 32])
                pib = PW[:, :, t + 1, 1:2].to_broadcast([128, 32, 32])
                tt(XB[:, 0], CC[:, 0], prb, ALU.mult)
                tt(TB, CC[:, 1], pib, ALU.mult)
                tt(XB[:, 0], XB[:, 0], TB, ALU.subtract)
                tt(XB[:, 1], CCn, prb, ALU.mult)
                tt(TB, CC[:, 0], pib, ALU.mult)
                tt(XB[:, 1], XB[:, 1], TB, ALU.subtract)
                P.op("act", lambda e: e.copy(WTs, XB), reads=[K], writes=["WTs"])
                P.dma("sp", wtd_d[l][:, t * 2048:(t + 1) * 2048], WTs.rearrange("p r a b -> p (r a b)"), reads=["WTs"], writes=[("WTd", l)])
            P.op("dve", lambda e: e.memset(CAR, 0.0), writes=[("CAR", c) for c in range(8)])
            P.barrier()
            import os as _os
            if _os.environ.get("S5DBG") == "prep":
                return
            for t in range(NT):
                ar = Arena(AR, base)
                hbuf = [ar.bf16(4096).rearrange("p (k t m) -> p k t m", t=8, m=64) for _ in range(2)]
                h = hbuf[t % 2]
                HK = ("h", t % 2)
                gT = ar.bf16(4096).rearrange("p (k n) -> p k n", n=512)
                sq = [ar.bf16(512) for _ in range(2)]
                rs = ar.f32(512)
                tmp = [ar.f32(512) for _ in range(2)]
                GTc = [ar.bf16(2048).rearrange("p (t r m) -> p t r m", t=8, r=2) for _ in range(2)]
                WTc = [ar.bf16(2048).rearrange("p (t r a b) -> p t r a b", t=8, r=2, a=4) for _ in range(2)]
                KTc = [ar.bf16(1024).rearrange("p (k b) -> p k b", b=128) for _ in range(2)]
                Sc = [ar.f32(512).rearrange("p (a r m) -> p a r m", a=4, r=2) for _ in range(2)]
                T1s = [ar.f32(512).rearrange("p (a r m) -> p a r m", a=4, r=2) for _ in range(2)]
                T2s = [ar.f32(512).rearrange("p (a r m) -> p a r m", a=4, r=2) for _ in range(2)]
                SBb = [ar.bf16(512).rearrange("p (a r m) -> p a r m", a=4, r=2) for _ in range(2)]
                cz = [ar.f32(8).rearrange("p (a r) -> p a r", r=2) for _ in range(2)]
                yt = [ar.f32(512) for _ in range(2)]
                mx, sg = yt[0], yt[1]
                wgl = [ar.bf16(1024).rearrange("p (k f) -> p k f", f=128) for _ in range(2)]
                ts = slice(t * TT, (t + 1) * TT)
                if t == 0:
                    norm_mod(0, lambda c: AA[:, l * 16 + c:l * 16 + c + 1], lambda c: mod_ap(l, 0, c), hbuf[0], sq, rs, tmp, perm=True, hkey=("h", 0))
                def ld(c):
                    b = c % 2
                    P.dma("sp", GTc[b], gtd_d[l][c].rearrange("p (t r m) -> p t r m", t=8, r=2), reads=[("GTd", l)], writes=[("GTc", b)])
                    P.dma("sp", WTc[b].rearrange("p t r a b -> p (t r) (a b)"), wtd_d[l].rearrange("p (tr ab) -> p tr ab", tr=16)[:, :, c * 128:(c + 1) * 128], reads=[("WTd", l)], writes=[("WTc", b)])
                    P.dma("sp", KTc[b], ktd_d[l][c].rearrange("p (k b) -> p k b", b=128), reads=[("KTd", l)], writes=[("KTc", b)])

                def sloc(c):
                    b = c % 2
                    S_ = Sc[b]
                    for pl in range(4):
                        rows = slice(32 * pl, 32 * pl + 32)
                        for r in range(2):
                            col = r * 64
                            for tau in range(8):
                                P.op("pe", lambda e: e.matmul(pb[pl][:, col:col + 64], GTc[b][rows, tau, r, :], h[rows, c, tau, :], start=(tau == 0), stop=(tau == 7), tile_position=(32 * pl, 0)),
                                     reads=[("GTc", b), HK], writes=[PB(pl)])
                    for pl in range(4):
                        P.op("act", lambda e: e.copy(S_[:, pl, :, :], pb[pl][:, 0:128].rearrange("p (r m) -> p r m", r=2)), reads=[PB(pl)], writes=[("Sre", b), ("Sim", b)])

                def ks(c):
                    b = c % 2
                    S_ = Sc[b]
                    sb_ = SBb[b]
                    ke = "dve" if b == 0 else "pool"
                    CK = ("CAR", c)
                    SK = [("Sre", b), ("Sim", b)]
                    cr4 = CAR[:, 4 * c:4 * c + 4, 0]
                    ci4 = CAR[:, 4 * c:4 * c + 4, 1]
                    czb = cz[b]
                    P.op(ke, lambda e: e.tensor_copy(sb_[:, :, :, 0], CAR[:, 4 * c:4 * c + 4, :]), reads=[CK], writes=[("SB", b)])
                    if t > 0:
                        car4 = CAR[:, 4 * c:4 * c + 4, :]
                        l0rb = LK[:, 4 * c:4 * c + 4, 0, 0:1].to_broadcast([128, 4, 2])
                        l0ib = LK[:, 4 * c:4 * c + 4, 0, 1:2].to_broadcast([128, 4, 2])
                        P.op(ke, lambda e: e.tensor_tensor(czb, car4, l0rb, ALU.mult), reads=[CK], writes=[("cz", b)])
                        P.op(ke, lambda e: e.tensor_tensor(S_[:, :, :, 0], S_[:, :, :, 0], czb, ALU.add), reads=[("cz", b)] + SK, writes=SK)
                        P.op(ke, lambda e: e.tensor_tensor(czb, car4, l0ib, ALU.mult), reads=[CK] + SK, writes=[("cz", b)])
                        P.op(ke, lambda e: e.tensor_tensor(S_[:, :, 0, 0], S_[:, :, 0, 0], czb[:, :, 1], ALU.subtract), reads=[("cz", b)] + SK, writes=SK)
                        P.op(ke, lambda e: e.tensor_tensor(S_[:, :, 1, 0], S_[:, :, 1, 0], czb[:, :, 0], ALU.add), reads=[("cz", b)] + SK, writes=SK)
                    T1, T2 = T1s[b], T2s[b]
                    for j in range(6):
                        d = 1 << j
                        n = 64 - d
                        lrb = LK[:, 4 * c:4 * c + 4, j, 0:1].unsqueeze(3).to_broadcast([128, 4, 2, n])
                        lib = LK[:, 4 * c:4 * c + 4, j, 1:2].unsqueeze(3).to_broadcast([128, 4, 2, n])
                        P.op(ke, lambda e: e.tensor_tensor(T1[:, :, :, 0:n], S_[:, :, :, 0:n], lrb, ALU.mult), reads=SK, writes=[("T1", b)])
                        P.op(ke, lambda e: e.tensor_tensor(T2[:, :, :, 0:n], S_[:, :, :, 0:n], lib, ALU.mult), reads=SK, writes=[("T2", b)])
                        P.op(ke, lambda e: e.tensor_tensor(S_[:, :, 0, d:64], S_[:, :, 0, d:64], T1[:, :, 0, 0:n], ALU.add), reads=[("T1", b), ("T2", b)], writes=[("Sre", b)])
                        P.op(ke, lambda e: e.tensor_tensor(S_[:, :, 0, d:64], S_[:, :, 0, d:64], T2[:, :, 1, 0:n], ALU.subtract), reads=[("T2", b)], writes=[("Sre", b)])
                        P.op(ke, lambda e: e.tensor_tensor(S_[:, :, 1, d:64], S_[:, :, 1, d:64], T1[:, :, 1, 0:n], ALU.add), reads=[("T1", b)], writes=[("Sim", b)])
                        P.op(ke, lambda e: e.tensor_tensor(S_[:, :, 1, d:64], S_[:, :, 1, d:64], T2[:, :, 0, 0:n], ALU.add), reads=[("T2", b)], writes=[("Sim", b)])
                    P.op(ke, lambda e: e.tensor_copy(CAR[:, 4 * c:4 * c + 4, :], S_[:, :, :, 63]), reads=SK + [("SB", b)], writes=[CK])
                    P.op("act", lambda e: e.copy(sb_[:, :, :, 1:64], S_[:, :, :, 0:63]), reads=SK, writes=[("SB", b)])

                def yout(c):
                    b = c % 2
                    sb_ = SBb[b]
                    yb = 4 + b
                    for tq_ in range(8):
                        for tau in range(tq_ + 1):
                            P.op("pe", lambda e: e.matmul(pb[yb][:, tq_ * 64:(tq_ + 1) * 64], KTc[b][:, tq_ - tau, :], h[:, c, tau, :], start=(tau == 0), stop=False),
                                 reads=[("KTc", b), HK], writes=[PB(yb)])
                        for pl in range(4):
                            rows = slice(32 * pl, 32 * pl + 32)
                            for r in range(2):
                                P.op("pe", lambda e: e.matmul(pb[yb][rows, tq_ * 64:(tq_ + 1) * 64], WTc[b][:, tq_, r, pl, :], sb_[:, pl, r, :], start=False, stop=(r == 1 and pl == 3), tile_position=(0, 32 * pl)),
                                     reads=[("WTc", b), ("SB", b)], writes=[PB(yb)])
                    ytb = yt[b]
                    P.op("dve", lambda e: e.scalar_tensor_tensor(ytb.rearrange("p (m t) -> p m t", t=8), h[:, c].rearrange("p t m -> p m t"), S5D[:, l * 8 + c:l * 8 + c + 1], pb[yb][:, :].rearrange("p (t m) -> p m t", t=8), ALU.mult, ALU.add), reads=[HK, PB(yb)], writes=[("yt", b)])
                    P.op("act", lambda e: e.activation(gT[:, c, :], ytb, AF.Gelu_apprx_tanh), reads=[("yt", b)], writes=["gT"])

                ld(0)
                sloc(0)
                for c in range(8):
                    if c + 1 < 8:
                        ld(c + 1)
                        sloc(c + 1)
                    ks(c)
                    yout(c)
                    if c == 4 and t + 1 < NT:
                        norm_mod(t + 1, lambda c_: AA[:, l * 16 + c_:l * 16 + c_ + 1], lambda c_: mod_ap(l, 0, c_), hbuf[(t + 1) % 2], sq, rs, tmp, perm=True, hkey=("h", (t + 1) % 2))
                wv = wglu_d[l].rearrange("(k p) f -> p k f", p=128)
                for fc in range(8):
                    b = fc % 2
                    P.dma("pool", wgl[b], wv[:, :, fc * 128:(fc + 1) * 128], writes=[("wgl", b)])
                    for k in range(8):
                        P.op("pe", lambda e, b=b, k=k: e.matmul(pb[6][:, :], wgl[b][:, k, :], gT[:, k, :], start=(k == 0), stop=(k == 7)), reads=[("wgl", b), "gT"], writes=[PB(6)])
                    P.op("act", lambda e, fc=fc: e.activation(sg, pb[6][:, :], AF.Sigmoid, bias=BGL[:, l * 8 + fc:l * 8 + fc + 1], scale=1.0), reads=[PB(6)], writes=[("yt", 1)])
                    P.op("dve", lambda e, fc=fc: e.tensor_tensor(mx, gT[:, fc, :], sg, ALU.mult), reads=["gT", ("yt", 1)], writes=[("yt", 0)])
                    P.op("dve", lambda e, fc=fc: e.scalar_tensor_tensor(xT[:, fc, ts], mx, mod_ap(l, 2, fc), xT[:, fc, ts], ALU.mult, ALU.add), reads=[("yt", 0), ("x", t, fc)], writes=[("x", t, fc)])
            P.barrier()
            if l + 1 in ffn_layers:
                cast_ffn(l + 1)
            if l in ffn_layers:
                for t in range(NT):
                    ffn(l, t, base)
                P.barrier()

        def kv_build(ckv, KTt, base):
            for t in range(NT):
                ar = Arena(AR, base)
                h = ar.bf16(4096).rearrange("p (k n) -> p k n", n=512)
                sq = [ar.bf16(512) for _ in range(2)]
                rs = ar.f32(512)
                rs2 = ar.f32(512)
                tmp = [ar.f32(512) for _ in range(2)]
                tab = ar.f32(1024).rearrange("p (r n) -> p r n", n=512)
                posi = ar.i32(512)
                angs = ar.f32(512)
                t1 = ar.f32(512)
                ti = ar.i32(512)
                wka = ar.bf16(8 * 288).rearrange("p (k f) -> p k f", f=288)
                wkr = ar.bf16(8 * 32).rearrange("p (k f) -> p k f", f=32)
                ts = slice(t * TT, (t + 1) * TT)
                R = slice(64, 96)
                if t == 0:
                    P.dma("pool", wka, wkva_d.rearrange("(k p) f -> p k f", p=128), writes=["wka"])
                    P.dma("pool", wkr, wkvar_d.rearrange("(k p) f -> p k f", p=128), writes=["wkr"])
                norm_mod(t, lambda c: AK[:, c:c + 1], lambda c: KVM[:, c:c + 1], h, sq, rs, tmp)
                for cc in range(2):
                    for k in range(8):
                        P.op("pe", lambda e, cc=cc, k=k: e.matmul(pb[cc][:, :], wka[:, k, cc * 128:(cc + 1) * 128], h[:, k, :], start=(k == 0), stop=(k == 7)), reads=["wka", "h"], writes=[PB(cc)])
                for k in range(8):
                    P.op("pe", lambda e, k=k: e.matmul(pb[2][R, :], wka[:, k, 256:288], h[:, k, :], start=(k == 0), stop=(k == 7), tile_position=(0, 64)), reads=["wka", "h"], writes=[PB(2)])
                for k in range(8):
                    P.op("pe", lambda e, k=k: e.matmul(pb[3][R, :], wkr[:, k, :], h[:, k, :], start=(k == 0), stop=(k == 7), tile_position=(0, 64)), reads=["wkr", "h"], writes=[PB(3)])
                for cc in range(2):
                    P.op("act", lambda e, cc=cc: e.activation(sq[cc], pb[cc][:, :], AF.Square), reads=[PB(cc)], writes=[("sq", cc)])
                    P.op("pe", lambda e, cc=cc: e.matmul(pb[4][:, :], ones256[:], sq[cc], start=(cc == 0), stop=(cc == 1)), reads=[("sq", cc), "ones256"], writes=[PB(4)])
                rstd_from_psum(4, rs2, key="rs2")
                for cc in range(2):
                    P.op("dve", lambda e, cc=cc: e.scalar_tensor_tensor(ckv[:, cc, ts], pb[cc][:, :], KVAG[:, cc:cc + 1], rs2, ALU.mult, ALU.mult), reads=[PB(cc), "rs2"], writes=["ckv"])
                P.dma("sp", posi[R, :], pos_d[64:96, ts], writes=["posi"])
                P.op("dve", lambda e: e.tensor_copy(angs[R, :], posi[R, :]), reads=["posi"], writes=["rsins"])
                P.op("dve", lambda e: e.tensor_scalar_mul(angs[R, :], angs[R, :], CST[R, 0:1]), reads=["rsins", "CST"], writes=["rsins"])
                rsin(tab[R, 1, :], angs[R, :], 512, t1, ti, prange=R)
                P.op("dve", lambda e: e.tensor_scalar_mul(tab[R, 1, :], tab[R, 1, :], CST[R, 1:2]), reads=["rsind"], writes=["tabs"])
                P.op("dve", lambda e: e.tensor_scalar_add(angs[R, :], angs[R, :], 0.5 * math.pi), reads=["rsins", "rsint"], writes=["rsins"])
                rsin(tab[R, 0, :], angs[R, :], 512, t1, ti, prange=R)
                P.dma("sp", tab_d[64:96, :, ts], tab[R, :, :], reads=["rsind", "tabs"], writes=["tabd"])
                P.op("act", lambda e: e.activation(sq[0][R, :], pb[2][R, :], AF.Square), reads=[PB(2)], writes=[("sq", 0)])
                P.op("pe", lambda e: e.matmul(pb[5][R, :], bq[R, 64:96], sq[0][R, :], start=True, stop=True, tile_position=(64, 64)), reads=[("sq", 0), "bq"], writes=[PB(5)])
                rstd_from_psum(5, rs, prange=R, key="rsr")
                P.op("dve", lambda e: e.scalar_tensor_tensor(tmp[0][R, :], pb[2][R, :], KG[R, 0:1], tab[R, 0, :], ALU.mult, ALU.mult), reads=[PB(2), "rsind"], writes=[("tmp", 0)])
                P.op("dve", lambda e: e.scalar_tensor_tensor(tmp[1][R, :], pb[3][R, :], KG[R, 1:2], tab[R, 1, :], ALU.mult, ALU.mult), reads=[PB(3), "tabs"], writes=[("tmp", 1)])
                P.op("dve", lambda e: e.tensor_tensor(tmp[0][R, :], tmp[0][R, :], tmp[1][R, :], ALU.add), reads=[("tmp", 0), ("tmp", 1)], writes=[("tmp", 0)])
                P.op("dve", lambda e: e.tensor_tensor(KTt[R, ts], tmp[0][R, :], rs[R, :], ALU.mult), reads=[("tmp", 0), "rsr"], writes=["KTr"])
                P.barrier()

        def mla_layer(l, ckv, KTt, base):
            jl = l - 2
            if mla_stage < 1:
                return
            ar = Arena(AR, base)
            qn = ar.bf16(8192).rearrange("p (c n) -> p c n", n=S)
            loc = ar.off
            wdq = ar.bf16(8 * 256).rearrange("p (k f) -> p k f", f=256)
            h = ar.bf16(4096).rearrange("p (k n) -> p k n", n=512)
            sq = [ar.bf16(512) for _ in range(2)]
            rs = ar.f32(512)
            rs2 = ar.f32(512)
            tmp = [ar.f32(512) for _ in range(2)]
            P.dma("pool", wdq, wdq_d[jl].rearrange("(k p) f -> p k f", p=128), writes=["wdq"])
            for t in range(NT):
                ts = slice(t * TT, (t + 1) * TT)
                norm_mod(t, lambda c: AA[:, l * 16 + c:l * 16 + c + 1], lambda c: mod_ap(l, 0, c), h, sq, rs, tmp)
                for cc in range(2):
                    for k in range(8):
                        P.op("pe", lambda e, cc=cc, k=k: e.matmul(pb[cc][:, :], wdq[:, k, cc * 128:(cc + 1) * 128], h[:, k, :], start=(k == 0), stop=(k == 7)), reads=["wdq", "h"], writes=[PB(cc)])
                for cc in range(2):
                    P.op("act", lambda e, cc=cc: e.activation(sq[cc], pb[cc][:, :], AF.Square), reads=[PB(cc)], writes=[("sq", cc)])
                    P.op("pe", lambda e, cc=cc: e.matmul(pb[4][:, :], ones256[:], sq[cc], start=(cc == 0), stop=(cc == 1)), reads=[("sq", cc), "ones256"], writes=[PB(4)])
                rstd_from_psum(4, rs2, key="rs2")
                for cc in range(2):
                    P.op("dve", lambda e, cc=cc, ts=ts: e.scalar_tensor_tensor(qn[:, cc, ts], pb[cc][:, :], QNG[:, jl * 2 + cc:jl * 2 + cc + 1], rs2, ALU.mult, ALU.mult), reads=[PB(cc), "rs2"], writes=["qn"])
            P.barrier()
            if mla_stage < 2:
                return
            ar = Arena(AR, loc)
            Vaug = ar.bf16(4096).rearrange("p (t d) -> p t d", d=128)
            OT = ar.bf16(4096)
            PT = [ar.bf16(512) for _ in range(3)]
            QT = [ar.bf16(512) for _ in range(2)]
            sqh = ar.bf16(512)
            rsq = ar.f32(512)
            rsk = rsq
            tq0 = ar.f32(512)
            rinv = ar.f32(512)
            tq1 = rinv
            tab = ar.f32(1024).rearrange("p (r n) -> p r n", n=512)
            wuq2 = [ar.bf16(2 * 96).rearrange("p (c f) -> p c f", f=96) for _ in range(2)]
            wuqr2 = [ar.bf16(2 * 32).rearrange("p (c f) -> p c f", f=32) for _ in range(2)]
            wkb2 = [ar.bf16(2 * 128).rearrange("p (c f) -> p c f", f=128) for _ in range(2)]
            wo = ar.bf16(1024)
            P.op("dve", lambda e: e.memset(Vaug[:, :, 64:128], 1.0), writes=["Vaug"])
            R = slice(64, 96)
            N_ = slice(0, 64)
            for hp in range(n_hp):
                for hh in range(2):
                    hd = 2 * hp + hh
                    P.dma("pool", wuq2[hh], wuq_d[jl].rearrange("(c p) f -> p c f", p=128)[:, :, hd * 96:(hd + 1) * 96], writes=[("wuq", hh)])
                    P.dma("pool", wuqr2[hh], wuqr_d[jl].rearrange("(c p) f -> p c f", p=128)[:, :, hd * 32:(hd + 1) * 32], writes=[("wuqr", hh)])
                    P.dma("pool", wkb2[hh], wkvb_d.rearrange("(c p) f -> p c f", p=128)[:, :, hd * 128:(hd + 1) * 128], writes=[("wkb", hh)])
                P.dma("pool", wo, wo_d[jl][hp * 128:(hp + 1) * 128, :], writes=["wo"])
                for hh in range(2):
                    hd = 2 * hp + hh
                    wuq, wuqr, wkb = wuq2[hh], wuqr2[hh], wkb2[hh]
                    WQ, WR, WK = ("wuq", hh), ("wuqr", hh), ("wkb", hh)
                    rsb = [(rsq, "rsq"), (tq0, "tq0")]
                    sqb = [(sqh, "sqh"), (PT[0], ("PT", 0))]

                    def vgroup(g):
                        for i in range(8):
                            tt_ = g * 8 + i
                            for cc in range(2):
                                P.op("pe", lambda e: e.matmul(pb[4][:, i * 64:(i + 1) * 64], ckv[:, cc, tt_ * 128:(tt_ + 1) * 128], wkb[:, cc, 64:128], start=(cc == 0), stop=(cc == 1)), reads=[WK, "ckv"], writes=[PB(4)])
                        P.op("act", lambda e: e.copy(Vaug[:, g * 8:(g + 1) * 8, 0:64], pb[4][:, :].rearrange("p (a b) -> p a b", b=64)), reads=[PB(4)], writes=["Vaug"])
                    for t in range(NT):
                        ts = slice(t * TT, (t + 1) * TT)
                        kb = t % 2
                        rsk_, rkey = rsb[kb]
                        sq_, skey = sqb[kb]
                        for cc in range(2):
                            P.op("pe", lambda e: e.matmul(pb[kb][N_, :], wkb[:, cc, 0:64], ckv[:, cc, ts], start=(cc == 0), stop=(cc == 1)), reads=[WK, "ckv"], writes=[PB(kb)])
                        P.op("act", lambda e: e.activation(sq_[N_, :], pb[kb][N_, :], AF.Square), reads=[PB(kb)], writes=[skey])
                        if t % 2 == 1:
                            vgroup(t // 2)
                        P.op("pe", lambda e: e.matmul(pb[2 + kb][N_, :], bq[N_, 0:64], sq_[N_, :], start=True, stop=True), reads=[skey, "bq"], writes=[PB(2 + kb)])
                        rstd_from_psum(2 + kb, rsk_, prange=N_, key=rkey, lntmp=rinv, lnkey="rinv")
                        P.op("dve", lambda e: e.scalar_tensor_tensor(KTt[N_, ts], pb[kb][N_, :], KG[N_, 0:1], rsk_[N_, :], ALU.mult, ALU.mult), reads=[PB(kb), rkey], writes=["KTn"])
                    def qprep(t):
                        ts = slice(t * TT, (t + 1) * TT)
                        qt_ = QT[t % 2]
                        QK_ = ("QT", t % 2)
                        P.dma("sp", tab[R, :, :], tab_d[64:96, :, ts], reads=["tabd"], writes=["tab"])
                        for cc in range(2):
                            P.op("pe", lambda e, cc=cc, ts=ts: e.matmul(pb[5][0:96, :], wuq[:, cc, :], qn[:, cc, ts], start=(cc == 0), stop=(cc == 1)), reads=[WQ, "qn"], writes=[PB(5)])
                        for cc in range(2):
                            P.op("pe", lambda e, cc=cc, ts=ts: e.matmul(pb[6][R, :], wuqr[:, cc, :], qn[:, cc, ts], start=(cc == 0), stop=(cc == 1), tile_position=(0, 64)), reads=[WR, "qn"], writes=[PB(6)])
                        P.op("act", lambda e: e.activation(sqh[0:96, :], pb[5][0:96, :], AF.Square), reads=[PB(5)], writes=["sqh"])
                        P.op("pe", lambda e: e.matmul(pb[7][0:96, :], bq[0:96, 0:96], sqh[0:96, :], start=True, stop=True), reads=["sqh", "bq"], writes=[PB(7)])
                        rstd_from_psum(7, rsq, prange=slice(0, 96), key="rsq", lntmp=tq0, lnkey="tq0")
                        P.op("dve", lambda e: e.scalar_tensor_tensor(qt_[N_, :], pb[5][N_, :], QG[N_, jl * 2:jl * 2 + 1], rsq[N_, :], ALU.mult, ALU.mult), reads=[PB(5), "rsq"], writes=[QK_])
                        P.op("dve", lambda e: e.scalar_tensor_tensor(tq0[R, :], pb[5][R, :], QG[R, jl * 2:jl * 2 + 1], tab[R, 0, :], ALU.mult, ALU.mult), reads=[PB(5), "tab"], writes=["tq0"])
                        P.op("dve", lambda e: e.scalar_tensor_tensor(tq1[R, :], pb[6][R, :], QG[R, jl * 2 + 1:jl * 2 + 2], tab[R, 1, :], ALU.mult, ALU.mult), reads=[PB(6), "tab"], writes=["rinv"])
                        P.op("dve", lambda e: e.tensor_tensor(tq0[R, :], tq0[R, :], tq1[R, :], ALU.add), reads=["tq0", "rinv"], writes=["tq0"])
                        P.op("dve", lambda e: e.tensor_tensor(qt_[R, :], tq0[R, :], rsq[R, :], ALU.mult), reads=["tq0", "rsq"], writes=[QK_])

                    if n_qt > 0:
                        qprep(0)
                    for t in range(n_qt):
                        ts = slice(t * TT, (t + 1) * TT)
                        if t + 1 < n_qt:
                            qprep(t + 1)
                        qt_ = QT[t % 2]
                        QK_ = ("QT", t % 2)
                        nk = 4 * t + 4
                        ob = 2 + t % 2

                        def qk(kt):
                            jd = kt - 4 * t
                            c0 = 128 * jd if jd > 0 else 0
                            sbk = (0, 1, 4)[kt % 3]
                            pt = PT[kt % 3]
                            P.op("pe", lambda e: e.matmul(pb[sbk][:, c0:512], KTt[0:96, kt * 128:(kt + 1) * 128], qt_[0:96, c0:512], start=True, stop=True), reads=["KTn", "KTr", QK_], writes=[PB(sbk)])
                            P.op("act", lambda e: e.activation(pt[:, c0:512], pb[sbk][:, c0:512], AF.Exp, scale=ATTN_SCALE), reads=[PB(sbk)], writes=[("PT", kt % 3)])
                            if jd >= 0:
                                P.op("pool", lambda e: e.memset(pt[64:128, c0:c0 + 64], 0.0), reads=[], writes=[("PT", kt % 3)])

                        def pv(kt):
                            jd = kt - 4 * t
                            c0 = 128 * jd if jd > 0 else 0
                            pt = PT[kt % 3]
                            P.op("pe", lambda e: e.matmul(pb[ob][:, c0:512], Vaug[:, kt, :], pt[:, c0:512], start=(kt == 0), stop=(kt == nk - 1)), reads=["Vaug", ("PT", kt % 3)], writes=[PB(ob)])
                        for kt in range(nk):
                            qk(kt)
                            if kt >= 1:
                                pv(kt - 1)
                        pv(nk - 1)
                        P.op("dve", lambda e: e.reciprocal(rinv[64:128, :], pb[ob][64:128, :]), reads=[PB(ob)], writes=["rinv"])
                        P.op("dve", lambda e: e.tensor_copy(rinv[0:64, :], rinv[64:128, :]), reads=["rinv"], writes=["rinv"])
                        P.op("dve", lambda e: e.tensor_tensor(OT[64 * hh:64 * hh + 64, ts], pb[ob][0:64, :], rinv[0:64, :], ALU.mult), reads=[PB(ob), "rinv"], writes=["OT"])
                for t in range(NT):
                    ts = slice(t * TT, (t + 1) * TT)
                    for dc in range(8):
                        ob = (4, 5, 0, 1)[dc % 4]
                        P.op("pe", lambda e, dc=dc, ob=ob, ts=ts: e.matmul(pb[ob][:, :], wo[:, dc * 128:(dc + 1) * 128], OT[:, ts], start=True, stop=True), reads=["wo", "OT"], writes=[PB(ob)])
                        if dc not in (1, 4, 6):
                            P.op("dve", lambda e, dc=dc, ob=ob, ts=ts: e.scalar_tensor_tensor(xT[:, dc, ts], pb[ob][:, :], mod_ap(l, 2, dc), xT[:, dc, ts], ALU.mult, ALU.add), reads=[PB(ob), ("x", t, dc)], writes=[("x", t, dc)])
                        else:
                            tb_, tk_ = [(tq0, "tq0"), (rinv, "rinv")][dc % 2]
                            P.op("act", lambda e, dc=dc, ob=ob, tb_=tb_: e.activation(tb_, pb[ob][:, :], AF.Copy, scale=mod_ap(l, 2, dc)), reads=[PB(ob)], writes=[tk_])
                            P.op("pool", lambda e, dc=dc, ts=ts, tb_=tb_: e.tensor_tensor(xT[:, dc, ts], xT[:, dc, ts], tb_, ALU.add), reads=[tk_, ("x", t, dc)], writes=[("x", t, dc)])
            P.barrier()
            if l == 2 and 3 in ffn_layers:
                cast_ffn(3)
            if l in ffn_layers:
                for t in range(NT):
                    ffn(l, t, base)
                P.barrier()

        for l in range(min(2, n_layers)):
            if s5_on:
                s5_layer(l)
        if n_layers > 2:
            ar = Arena(AR)
            ckv = ar.bf16(8192).rearrange("p (c n) -> p c n", n=S)
            KTt = ar.bf16(4096)
            base2 = ar.off
            P.barrier()
            kv_build(ckv, KTt, base2)
            for l in range(2, n_layers):
                mla_layer(l, ckv, KTt, base2)

        P.barrier()
        ar = Arena(AR, 6200)
        stg = [ar.f32(1024) for _ in range(2)]
        for t in range(32):
            b = t % 2
            for hf in range(2):
                bk = 2 + b * 2 + hf
                for j in range(4):
                    c = hf * 4 + j
                    P.op("pe", lambda e, bk=bk, j=j, c=c, t=t: e.matmul(pb[bk][:, j * 128:(j + 1) * 128], xT[:, c, t * 128:(t + 1) * 128], ident[:], start=True, stop=True, is_transpose=True),
                         reads=[("x", t // 4, c), "ident"], writes=[PB(bk)])
                if hf == 0:
                    P.op("dve", lambda e, bk=bk, b=b: e.tensor_copy(stg[b][:, 0:512], pb[bk][:, :]), reads=[PB(bk)], writes=[("stg", b)])
                else:
                    P.op("act", lambda e, bk=bk, b=b: e.copy(stg[b][:, 512:1024], pb[bk][:, :]), reads=[PB(bk)], writes=[("stg", b)])
            P.dma("sp", out_d[t * 128:(t + 1) * 128, :], stg[b], reads=[("stg", b)], is_output=True)
        P.finish()
        build.last_counts = dict(P.cnt)
        build.sbuf_left = nc.sbuf_bytes_remaining
    return nc


def _fm(v, nchunk):
    return np.ascontiguousarray(np.asarray(v, np.float32).reshape(nchunk, 128).T)


def _host_layout(inp):
    f = lambda a: np.ascontiguousarray(np.asarray(a, dtype=np.float32))
    sh = {}
    sh["ada_w"] = f(inp["ada_w"])
    sh["ada_b"] = np.ascontiguousarray(np.concatenate([_fm(inp["ada_b"][l], 48) for l in range(4)], axis=1))
    g12 = []
    for l in range(4):
        g12 += [_fm(inp["norm1_g"][l], 8), _fm(inp["norm2_g"][l], 8)]
    sh["g12"] = np.ascontiguousarray(np.concatenate(g12, axis=1))
    sh["wg"] = f(inp["ffn_w_gate"]); sh["wu"] = f(inp["ffn_w_up"]); sh["wd"] = f(inp["ffn_w_down"])
    s5p = np.zeros((2, 128, 96), np.float32)
    s5b = np.zeros((2, 128, 2, 32, 32), np.float32)
    s5c = np.zeros((2, 128, 2, 32, 32), np.float32)
    for l in range(2):
        lre = np.asarray(inp["s5_lam_re"][l]).reshape(32, 2, 64)
        lim = np.asarray(inp["s5_lam_im"][l]).reshape(32, 2, 64)
        ldt = np.broadcast_to(np.asarray(inp["s5_log_dt"][l]).reshape(32, 2, 1), (32, 2, 64))
        s5p[l, :, 0:32] = lre.transpose(1, 2, 0).reshape(128, 32)
        s5p[l, :, 32:64] = lim.transpose(1, 2, 0).reshape(128, 32)
        s5p[l, :, 64:96] = ldt.transpose(1, 2, 0).reshape(128, 32)
        for r, (bk, ck) in enumerate([("s5_b_re", "s5_c_re"), ("s5_b_im", "s5_c_im")]):
            b = np.asarray(inp[bk][l]).reshape(32, 2, 64, 16)
            c = np.asarray(inp[ck][l]).reshape(32, 2, 16, 64)
            for g2 in range(2):
                s5b[l, g2 * 64:(g2 + 1) * 64, r, :, g2 * 16:(g2 + 1) * 16] = b[:, g2].transpose(1, 0, 2)
                s5c[l, g2 * 64:(g2 + 1) * 64, r, :, g2 * 16:(g2 + 1) * 16] = c[:, g2].transpose(2, 0, 1)
    sh["s5p"] = s5p; sh["s5b"] = s5b.reshape(2, 128, 2048); sh["s5c"] = s5c.reshape(2, 128, 2048)
    sh["s5d"] = np.ascontiguousarray(np.concatenate([_fm(inp["s5_d"][l], 8) for l in range(2)], axis=1))
    sh["wglu"] = f(inp["s5_w_glu"])
    sh["bglu"] = np.ascontiguousarray(np.concatenate([_fm(inp["s5_b_glu"][l], 8) for l in range(2)], axis=1))
    sh["kvaw"] = f(inp["kv_ada_w"]); sh["kvab"] = _fm(inp["kv_ada_b"], 16); sh["kvg"] = _fm(inp["kv_norm_g"], 8)
    wkva = f(inp["w_kv_a"])
    sh["wkva"] = wkva
    perm = (np.arange(32) + 16) % 32
    sh["wkvar"] = np.ascontiguousarray(wkva[:, 256 + perm])
    sh["kvag"] = _fm(inp["kv_a_norm_g"], 2)
    sh["wkvb"] = f(inp["w_kv_b"])
    kg = np.zeros((96, 2), np.float32)
    kg[0:64, 0] = np.asarray(inp["k_nope_norm_g"]); kg[64:96, 0] = np.asarray(inp["k_rope_norm_g"]); kg[64:96, 1] = np.asarray(inp["k_rope_norm_g"])[perm]
    sh["kg"] = kg
    sh["wdq"] = f(inp["mla_w_dq"])
    sh["qng"] = np.ascontiguousarray(np.concatenate([_fm(inp["mla_q_norm_g"][j], 2) for j in range(2)], axis=1))
    wuq = f(inp["mla_w_uq"])
    sh["wuq"] = wuq
    cols = np.concatenate([h * 96 + 64 + perm for h in range(16)])
    sh["wuqr"] = np.ascontiguousarray(wuq[:, :, cols])
    qg = np.zeros((96, 4), np.float32)
    for j in range(2):
        qg[0:64, 2 * j] = np.asarray(inp["mla_q_nope_norm_g"][j]); qg[64:96, 2 * j] = np.asarray(inp["mla_q_rope_norm_g"][j])
        qg[64:96, 2 * j + 1] = np.asarray(inp["mla_q_rope_norm_g"][j])[perm]
    sh["qg"] = qg
    sh["wo"] = f(inp["mla_w_o"])
    sh["ident"] = np.eye(128, dtype=np.float32)
    bq = np.zeros((96, 96), np.float32); bq[0:64, 0:64] = 1.0 / 64.0; bq[64:96, 64:96] = 1.0 / 32.0
    sh["bq"] = bq
    cst = np.zeros((96, 2), np.float32)
    k = np.arange(32) % 16
    cst[64:96, 0] = (1.0 / (10000.0 ** (np.arange(0, 32, 2, dtype=np.float32) / 32.0)))[k]
    cst[64:96, 1] = np.where(np.arange(32) < 16, -1.0, 1.0)
    sh["cst"] = cst
    return sh


_NC_CACHE = {}


def kernel(**inputs):
    x = np.asarray(inputs["x"], np.float32)
    c = np.asarray(inputs["c"], np.float32)
    pos = np.asarray(inputs["positions"], np.int32)
    sh = _host_layout(inputs)
    if "nc" not in _NC_CACHE:
        _NC_CACHE["nc"] = build()
    nc = _NC_CACHE["nc"]
    in_maps = []
    for b in range(8):
        m = dict(sh)
        m["x"] = np.ascontiguousarray(x[b])
        m["cT"] = _fm(c[b], 8)
        m["pos"] = np.ascontiguousarray(np.broadcast_to(pos[b][None, :], (96, S))).astype(np.int32)
        in_maps.append(m)
    res = run_bass_kernel_spmd(nc, in_maps, core_ids=list(range(8)))
    return np.stack([np.asarray(r["out"], np.float32) for r in res.results], axis=0)
```
